# Optimizing a Trainium2 kernel written in Bass

```python
import math
import jax, jax.numpy as jnp
from jax import lax
import numpy as np

D_MODEL = 1024
BATCH = 8
SEQ = 2048
DEPTH = 4

GRID_W = 64
WIN_ROWS = 8
WIN_COLS = 16
NA_HEADS = 8
HEAD_DIM = 64
NA_WIDTH = NA_HEADS * HEAD_DIM
CONV_WIDTH = 512
CONV_TAPS = 3
MIX_IN = 3 * NA_WIDTH + 3 * CONV_WIDTH
MIX_WIDTH = NA_WIDTH + CONV_WIDTH
FOURIER_GROUPS = 4
N_EXPERTS = 16
EXPERT_FF = 1024
CAPACITY_FACTOR = 2
N_EVEN = (DEPTH + 1) // 2
N_ODD = DEPTH // 2
N_MOD = 6
RMS_EPS = 1e-6

kernel_name = "hybrid_natten_shortconv_fnet_ecmoe_encoder"


def _rmsnorm(x, g):
    xf = x.astype(jnp.float32)
    y = xf * lax.rsqrt(jnp.mean(xf * xf, axis=-1, keepdims=True) + RMS_EPS)
    return (y * g.astype(jnp.float32)).astype(x.dtype)


def _neighbourhood_attention(q, k, v, rpb):
    b, h, s, hd = q.shape
    rows = s // GRID_W
    kr = min(WIN_ROWS, rows)
    kc = WIN_COLS
    n_keys = kr * kc
    r = jnp.arange(rows, dtype=jnp.int32)
    col = jnp.arange(GRID_W, dtype=jnp.int32)
    rs = jnp.clip(r - kr // 2, 0, rows - kr)
    cs = jnp.clip(col - kc // 2, 0, GRID_W - kc)
    key_r = rs[:, None] + jnp.arange(kr, dtype=jnp.int32)
    key_c = cs[:, None] + jnp.arange(kc, dtype=jnp.int32)
    idx = key_r[:, None, :, None] * GRID_W + key_c[None, :, None, :]
    idx = idx.reshape(rows, GRID_W, n_keys)
    dr = key_r - r[:, None]
    dc = key_c - col[:, None]
    bias = rpb[:, dr[:, None, :, None] + (WIN_ROWS - 1), dc[None, :, None, :] + (WIN_COLS - 1)]
    bias = bias.reshape(h, rows, GRID_W, n_keys).transpose(1, 0, 2, 3)
    q_rows = q.reshape(b, h, rows, GRID_W, hd).transpose(2, 0, 1, 3, 4)
    scale = 1.0 / math.sqrt(hd)

    def row_block(args):
        q_blk, idx_blk, bias_blk = args
        k_blk = k[:, :, idx_blk]
        v_blk = v[:, :, idx_blk]
        sc = jnp.einsum('bhqd,bhqnd->bhqn', q_blk, k_blk).astype(jnp.float32) * scale
        sc = sc + bias_blk.astype(jnp.float32)[None]
        p = jax.nn.softmax(sc, axis=-1).astype(v.dtype)
        return jnp.einsum('bhqn,bhqnd->bhqd', p, v_blk)

    out = lax.map(row_block, (q_rows, idx, bias))
    return out.transpose(1, 2, 0, 3, 4).reshape(b, h, s, hd)


def _short_gated_conv(bg, cg, xv, conv_w):
    u = cg * xv
    u_prev = jnp.pad(u, ((0, 0), (1, 0), (0, 0)))[:, :-1]
    u_next = jnp.pad(u, ((0, 0), (0, 1), (0, 0)))[:, 1:]
    z = u_prev * conv_w[0] + u * conv_w[1] + u_next * conv_w[2]
    return bg * z


def _even_mixer(h, w_in, qk_g, rpb, conv_w, w_out):
    b, s, _ = h.shape
    proj = h @ w_in
    q, k, v, bg, cg, xv = jnp.split(
        proj,
        [NA_WIDTH, 2 * NA_WIDTH, 3 * NA_WIDTH, 3 * NA_WIDTH + CONV_WIDTH, 3 * NA_WIDTH + 2 * CONV_WIDTH],
        axis=-1)
    q = _rmsnorm(q.reshape(b, s, NA_HEADS, HEAD_DIM), qk_g[0]).transpose(0, 2, 1, 3)
    k = _rmsnorm(k.reshape(b, s, NA_HEADS, HEAD_DIM), qk_g[1]).transpose(0, 2, 1, 3)
    v = v.reshape(b, s, NA_HEADS, HEAD_DIM).transpose(0, 2, 1, 3)
    attn = _neighbourhood_attention(q, k, v, rpb)
    attn = attn.transpose(0, 2, 1, 3).reshape(b, s, NA_WIDTH)
    conv = _short_gated_conv(bg, cg, xv, conv_w)
    return jnp.concatenate([attn, conv], axis=-1) @ w_out


def _fourier_mixer(h, w_out):
    b, s, d = h.shape
    hg = h.astype(jnp.float32).reshape(b, s, FOURIER_GROUPS, d // FOURIER_GROUPS)
    f = jnp.real(jnp.fft.fft2(hg, axes=(1, 3), norm="ortho"))
    return f.reshape(b, s, d).astype(h.dtype) @ w_out


def _expert_choice_moe(h, router_w, w_gate, w_up, w_down):
    b, s, d = h.shape
    cap = max(1, CAPACITY_FACTOR * s // N_EXPERTS)
    logits = (h @ router_w).astype(jnp.float32)
    aff = jax.nn.softmax(logits, axis=-1)
    gate, tok = lax.top_k(aff.transpose(0, 2, 1), cap)
    bidx = jnp.arange(b, dtype=jnp.int32)[:, None, None]
    xe = h[bidx, tok]
    a = jnp.einsum('becd,edf->becf', xe, w_gate)
    u = jnp.einsum('becd,edf->becf', xe, w_up)
    y = jnp.einsum('becf,efd->becd', jax.nn.silu(a) * u, w_down)
    y = y * gate[..., None].astype(h.dtype)
    return jnp.zeros_like(h).at[bidx, tok].add(y)


def setup_inputs(seed: int = 0) -> dict:
    key = jax.random.key(seed)
    ks = jax.random.split(key, 16)
    f32 = jnp.float32
    d = D_MODEL

    def nrm(k, shape, scale):
        return jax.random.normal(k, shape, f32) * scale

    return {
        "x": nrm(ks[0], (BATCH, SEQ, d), 1.0),
        "c": nrm(ks[1], (BATCH, d), 1.0),
        "mod_w": nrm(ks[2], (DEPTH, d, N_MOD * d), 0.5 * d ** -0.5),
        "mod_b": nrm(ks[3], (DEPTH, N_MOD * d), 0.02),
        "norm_g": 1.0 + nrm(ks[4], (DEPTH, 2, d), 0.02),
        "mix_w_in": nrm(ks[5], (N_EVEN, d, MIX_IN), d ** -0.5),
        "qk_g": 1.0 + nrm(ks[6], (N_EVEN, 2, HEAD_DIM), 0.02),
        "rpb": nrm(ks[7], (N_EVEN, NA_HEADS, 2 * WIN_ROWS - 1, 2 * WIN_COLS - 1), 0.1),
        "conv_w": nrm(ks[8], (N_EVEN, CONV_TAPS, CONV_WIDTH), CONV_TAPS ** -0.5),
        "mix_w_out": nrm(ks[9], (N_EVEN, MIX_WIDTH, d), MIX_WIDTH ** -0.5),
        "fnet_w_out": nrm(ks[10], (N_ODD, d, d), d ** -0.5),
        "router_w": nrm(ks[11], (DEPTH, d, N_EXPERTS), d ** -0.5),
        "exp_w_gate": nrm(ks[12], (DEPTH, N_EXPERTS, d, EXPERT_FF), d ** -0.5),
        "exp_w_up": nrm(ks[13], (DEPTH, N_EXPERTS, d, EXPERT_FF), d ** -0.5),
        "exp_w_down": nrm(ks[14], (DEPTH, N_EXPERTS, EXPERT_FF, d), EXPERT_FF ** -0.5),
    }


def reference(x, c, mod_w, mod_b, norm_g, mix_w_in, qk_g, rpb, conv_w, mix_w_out,
              fnet_w_out, router_w, exp_w_gate, exp_w_up, exp_w_down):
    sc = jax.nn.silu(c)
    for l in range(DEPTH):
        mod = sc @ mod_w[l] + mod_b[l]
        sh1, sc1, g1, sh2, sc2, g2 = [m[:, None, :] for m in jnp.split(mod, N_MOD, axis=-1)]
        h = _rmsnorm(x, norm_g[l, 0]) * (1.0 + sc1) + sh1
        if l % 2 == 0:
            j = l // 2
            m = _even_mixer(h, mix_w_in[j], qk_g[j], rpb[j], conv_w[j], mix_w_out[j])
        else:
            m = _fourier_mixer(h, fnet_w_out[l // 2])
        x = x + g1 * m
        h = _rmsnorm(x, norm_g[l, 1]) * (1.0 + sc2) + sh2
        x = x + g2 * _expert_choice_moe(h, router_w[l], exp_w_gate[l], exp_w_up[l], exp_w_down[l])
    return x
```

```python
from contextlib import ExitStack

import numpy as np
import ml_dtypes

import concourse.bass as bass
import concourse.mybir as mybir
from concourse.bass_utils import run_bass_kernel_spmd

F32 = mybir.dt.float32
BF16 = mybir.dt.bfloat16
I32 = mybir.dt.int32
U32 = mybir.dt.uint32
AF = mybir.ActivationFunctionType
ALU = mybir.AluOpType
AX = mybir.AxisListType

S = 2048
D = 1024
NT = 16
KC = 8
DEPTH = 4
NE = 16
CAP = 256
EPS = 1e-6
NEG = -30000.0
NO_MOD_RING = False
MERGE_POOLS = True

ENGS = ("pe", "act", "dve", "pool", "sp")


class _Op:
    __slots__ = ("eng", "emit", "deps", "signal", "dma", "sem", "val", "barrier")

    def __init__(self, eng, emit, dma=False):
        self.eng = eng
        self.emit = emit
        self.deps = []
        self.signal = False
        self.dma = dma
        self.sem = None
        self.val = 0
        self.barrier = False


class _Res:
    __slots__ = ("writer", "readers")

    def __init__(self):
        self.writer = None
        self.readers = []


class SemCtx:
    def __init__(self, nc, es, n_sw=2, n_hw=64):
        self.esem = {e: es.enter_context(nc.semaphore("s_" + e)) for e in ENGS}
        self.n = {"sw": n_sw, "hw": n_hw}
        self.dsem = {"sw": [es.enter_context(nc.semaphore("dsw%d" % i)) for i in range(n_sw)],
                     "hw": [es.enter_context(nc.semaphore("dhw%d" % i)) for i in range(n_hw)]}
        self.ecount = {e: 0 for e in ENGS}
        self.dcount = {"sw": [0] * n_sw, "hw": [0] * n_hw}


class Prog:
    def __init__(self, nc, ctx):
        self.nc = nc
        self.ctx = ctx
        self.ops = {e: [] for e in ENGS}
        self.res = {}
        self.key2sem = {}
        self.sem_last = {}

    def _collect(self, o, reads, writes):
        deps = []
        for r in reads:
            st = self.res.get(r)
            if st is not None and st.writer is not None:
                deps.append((st.writer, "raw"))
        for w in writes:
            st = self.res.get(w)
            if st is not None:
                if st.writer is not None:
                    deps.append((st.writer, "waw"))
                for rd in st.readers:
                    deps.append((rd, "war"))
        seen = set()
        for d, kind in deps:
            if d is o or id(d) in seen:
                continue
            if (not d.dma) and (not o.dma) and d.eng == o.eng:
                if kind != "raw" or o.eng == "pe":
                    continue
            seen.add(id(d))
            o.deps.append(d)
            if not d.dma:
                d.signal = True
        for r in reads:
            self.res.setdefault(r, _Res()).readers.append(o)
        for w in writes:
            st = self.res.setdefault(w, _Res())
            st.writer = o
            st.readers = []

    def add(self, eng, emit, reads=(), writes=(), after=()):
        o = _Op(eng, emit)
        self._collect(o, reads, writes)
        for d in after:
            if d is not None and all(d is not x for x in o.deps):
                o.deps.append(d)
                if not d.dma:
                    d.signal = True
        self.ops[eng].append(o)
        return o

    def dma(self, eng, emit, reads=(), write=None, after=()):
        o = _Op(eng, emit, dma=True)
        self._collect(o, reads, (write,))
        for d in after:
            if d is not None and all(d is not x for x in o.deps):
                o.deps.append(d)
                if not d.dma:
                    d.signal = True
        pool = "sw" if (eng == "pool" and not MERGE_POOLS) else "hw"
        if write not in self.key2sem:
            n_used = sum(1 for v in self.key2sem.values() if v[0] == pool)
            assert n_used < self.ctx.n[pool], "too many DMA keys in one phase (%s)" % pool
            self.key2sem[write] = (pool, n_used)
        s = self.key2sem[write]
        assert s[0] == pool, ("DMA key used from both DGE kinds", write)
        prev = self.sem_last.get(s)
        if prev is not None and all(prev is not d for d in o.deps):
            o.deps.append(prev)
        self.ctx.dcount[s[0]][s[1]] += 16
        o.sem = s
        o.val = self.ctx.dcount[s[0]][s[1]]
        assert o.val < 65000
        self.sem_last[s] = o
        self.ops[eng].append(o)
        return o

    def finish(self):
        lasts = []
        for e in ENGS:
            for o in reversed(self.ops[e]):
                if not o.dma:
                    lasts.append(o)
                    break
        dmas = list(self.sem_last.values())
        for e in ENGS:
            o = _Op(e, None)
            o.barrier = True
            for d in lasts:
                if d.eng != e:
                    o.deps.append(d)
                    d.signal = True
            o.deps.extend(dmas)
            self.ops[e].append(o)
        self._emit()

    def _emit(self):
        nc, ctx = self.nc, self.ctx
        for e in ENGS:
            c = ctx.ecount[e]
            for o in self.ops[e]:
                if o.dma or o.barrier:
                    continue
                if o.signal:
                    c += 1
                    o.val = c
            assert c < 65000
            ctx.ecount[e] = c

        def run(e, eng):
            seen = {}
            for o in self.ops[e]:
                for d in o.deps:
                    if d.dma:
                        key, sem = ("d", d.sem), ctx.dsem[d.sem[0]][d.sem[1]]
                    else:
                        key, sem = ("e", d.eng), ctx.esem[d.eng]
                    if seen.get(key, 0) >= d.val:
                        continue
                    seen[key] = d.val
                    eng.wait_ge(sem, d.val)
                if o.barrier:
                    continue
                ins = o.emit(eng)
                if o.dma:
                    ins.then_inc(ctx.dsem[o.sem[0]][o.sem[1]], 16)
                elif o.signal:
                    ins.then_inc(ctx.esem[e], 1)

        with nc.Block() as block:
            @block.tensor
            def _(eng):
                run("pe", eng)

            @block.scalar
            def _(eng):
                run("act", eng)

            @block.vector
            def _(eng):
                run("dve", eng)

            @block.gpsimd
            def _(eng):
                run("pool", eng)

            @block.sync
            def _(eng):
                run("sp", eng)


class RR:
    def __init__(self, items):
        self.items = list(items)
        self.i = 0

    def next(self):
        x = self.items[self.i % len(self.items)]
        self.i += 1
        return x


_ISZ = {F32: 4, BF16: 2, I32: 4, U32: 4}


class Arena:
    def __init__(self, ap, nbytes):
        self.ap = ap
        self.nbytes = nbytes

    def view(self, off, shape, dt):
        n = 1
        for s_ in shape[1:]:
            n *= s_
        nb = n * _ISZ[dt]
        assert off % 4 == 0 and off + nb <= self.nbytes, (off, nb, self.nbytes)
        v = self.ap[0:shape[0], off // 2:(off + nb) // 2]
        if dt != BF16:
            v = v.bitcast(dt)
        if len(shape) == 3:
            v = v.rearrange("p (a b) -> p a b", b=shape[2])
        elif len(shape) == 4:
            v = v.rearrange("p (a b c) -> p a b c", b=shape[2], c=shape[3])
        return v


class Bump:
    def __init__(self, arena, lo, hi):
        self.arena, self.cur, self.hi = arena, lo, hi

    def alloc(self, shape, dt):
        n = 1
        for s_ in shape[1:]:
            n *= s_
        nb = (n * _ISZ[dt] + 63) // 64 * 64
        off = self.cur
        assert off + nb <= self.hi, ("arena overflow", off, nb, self.hi)
        self.cur += nb
        return self.arena.view(off, shape, dt)


def _mm(p, out, pairs, reads, writes):
    def f(e):
        n = len(pairs)
        ins = None
        for i, (l, r) in enumerate(pairs):
            ins = e.matmul(out, lhsT=l, rhs=r, start=(i == 0), stop=(i == n - 1))
        return ins
    p.add("pe", f, reads, writes)


def _rs(r):
    return min(max(r - 4, 0), 24)


def _cs(c):
    return min(max(c - 8, 0), 48)


ATT_CLS_T = [0, 1, 2, 14, 15]


def _att_cls(t):
    return {0: 0, 1: 1, 14: 3, 15: 4}.get(t, 2)


def _att_chunks(t):
    c0 = (_rs(2 * t) * 64) // 128
    c1 = ((_rs(2 * t + 1) + 8) * 64 - 1) // 128
    return c0, c1 - c0 + 1


def _bias_index():
    dr = np.zeros((5, 128, 5, 128), np.int64)
    dc = np.zeros((5, 128, 5, 128), np.int64)
    ok = np.zeros((5, 128, 5, 128), bool)
    for ci, t in enumerate(ATT_CLS_T):
        c0, nk = _att_chunks(t)
        q = np.arange(128)
        qt = t * 128 + q
        r, c = qt // 64, qt % 64
        rs = np.clip(r - 4, 0, 24)
        cs = np.clip(c - 8, 0, 48)
        for j in range(nk):
            pk = np.arange(128)
            kt = (c0 + j) * 128 + pk
            kr, kc = kt // 64, kt % 64
            inside = ((kr[:, None] >= rs[None, :]) & (kr[:, None] < rs[None, :] + 8) &
                      (kc[:, None] >= cs[None, :]) & (kc[:, None] < cs[None, :] + 16))
            ok[ci, :, j, :] = inside
            dr[ci, :, j, :] = np.where(inside, kr[:, None] - r[None, :] + 7, 0)
            dc[ci, :, j, :] = np.where(inside, kc[:, None] - c[None, :] + 15, 0)
    return dr, dc, ok


def build_program(phases=None, debug_out=None):
    nc = bass.Bass("TRN2", target_bir_lowering=False)
    dt_in = lambda name, shape, dt=F32: nc.dram_tensor(name, shape, dt, kind="ExternalInput").ap()
    x_d = dt_in("x", [S, D])
    cT_d = dt_in("cT", [128, 8])
    modw_d = dt_in("mod_w", [DEPTH, D, 6 * D])
    modb_d = dt_in("mod_b", [DEPTH, 6 * D])
    ng_d = dt_in("norm_g", [DEPTH * 2, D])
    win_d = dt_in("mix_w_in", [2, D, 3 * D])
    wout_d = dt_in("mix_w_out", [2, D, D])
    fw_d = dt_in("fnet_w_out", [2, D, D])
    qkg_d = dt_in("qkg", [2, 128, 2])
    cw_d = dt_in("cw", [2, 128, 12])
    btab_d = dt_in("btab", [2, 8, 128, 3200])
    rw_d = dt_in("router_w", [DEPTH, D, NE])
    wg_d = dt_in("exp_w_gate", [DEPTH, NE, D, D])
    wu_d = dt_in("exp_w_up", [DEPTH, NE, D, D])
    wd_d = dt_in("exp_w_down", [DEPTH, NE, D, D])
    dcos_d = dt_in("dft_cos", [S, S], BF16)
    dsin_d = dt_in("dft_sin", [S, S], BF16)
    cc_d = dt_in("dft_cc", [256, 256], BF16)
    cs_d = dt_in("dft_csn", [256, 256], BF16)
    nyq_d = dt_in("dft_nyq", [128, NT], BF16)
    out_d = nc.dram_tensor("out", [S, D], F32, kind="ExternalOutput").ap()
    modv_d = nc.dram_tensor("modv", [DEPTH, 6 * D], F32, kind="Internal").ap()
    hdr_d = nc.dram_tensor("h_dram", [S, D], BF16, kind="Internal").ap()
    acc_ds = [nc.dram_tensor("acc_dram%d" % i, [S, D], F32, kind="Internal").ap() for i in range(2)]
    wbf_d = nc.dram_tensor("wbf", [NE * 3, D, D], BF16, kind="Internal").ap()
    PC_EXPERTS = {0: 8, 1: 4, 2: 8, 3: 4}
    PC_SPLIT = {0: (8, 12, 4, 0), 1: (4, 8), 2: (8, 12, 4, 0), 3: (4, 8)}

    if phases is None:
        phases = [("mod",)]
        for l in range(DEPTH):
            phases.append(("mix", l))
            phases.append(("moe", l))

    with ExitStack() as es:
        ARENA_BYTES = 206 * 1024
        arena_t = es.enter_context(nc.sbuf_tensor("arena", [128, ARENA_BYTES // 2], BF16))
        AR = Arena(arena_t[:], ARENA_BYTES)
        pbanks = [es.enter_context(nc.psum_tensor("pb%d" % i, [128, 512], F32)) for i in range(8)]
        ctx = SemCtx(nc, es)

        def PSf(i, shape=None):
            v = pbanks[i][:]
            if shape is not None and len(shape) == 3:
                n = shape[1] * shape[2]
                v = v[0:shape[0], 0:n].rearrange("p (a b) -> p a b", b=shape[2])
            elif shape is not None:
                v = v[0:shape[0], 0:shape[1]]
            return v

        def PSb(i, shape):
            v = pbanks[i][:].bitcast(BF16)
            n = 1
            for s_ in shape[1:]:
                n *= s_
            v = v[0:shape[0], 0:n]
            if len(shape) == 3:
                v = v.rearrange("p (a b) -> p a b", b=shape[2])
            return v

        pers = Bump(AR, 0, 80 * 1024)
        xs = pers.alloc([128, NT, D], F32)
        mvA = pers.alloc([128, D], F32)
        mvB = pers.alloc([128, D], F32)
        mvC = pers.alloc([128, D], F32)
        ident = pers.alloc([128, 128], BF16)
        identf = pers.alloc([128, 128], F32)
        bones = pers.alloc([128, 128], BF16)
        epst = pers.alloc([128, 1], F32)
        sc12 = pers.alloc([128, KC, 12, 12], BF16)
        PH0 = 80 * 1024
        PHN = ARENA_BYTES

        xs_v = x_d.rearrange("(t p) d -> p t d", p=128)
        out_v = out_d.rearrange("(t p) d -> p t d", p=128)

        def phase_mod():
            p = Prog(nc, ctx)
            b = Bump(AR, PH0, PHN)
            scol = b.alloc([128, 8], F32)
            scolb = b.alloc([128, 8], BF16)
            wsl = [b.alloc([128, KC, 512], BF16) for _ in range(4)]
            for q4 in range(4):
                p.dma("sp", lambda e, q4=q4: e.dma_start(out=xs[:, q4 * 4:(q4 + 1) * 4, :], in_=xs_v[:, q4 * 4:(q4 + 1) * 4, :]),
                      write=("xs", q4))
            p.add("pool", lambda e: e.memset(identf, 0.0), writes=["identf"])
            p.add("pool", lambda e: e.affine_select(out=identf, in_=identf, pattern=[[-1, 128]], compare_op=ALU.not_equal,
                                                   fill=1.0, base=0, channel_multiplier=1),
                  reads=["identf"], writes=["identf"])
            p.add("dve", lambda e: e.tensor_copy(out=ident, in_=identf), reads=["identf"], writes=["ident"])
            p.add("pool", lambda e: e.memset(epst, EPS), writes=["epst"])
            p.add("pool", lambda e: e.memset(bones, 0.0), writes=["bones"])
            p.add("pool", lambda e: e.memset(bones[0:64, 0:64], 1.0), reads=["bones"], writes=["bones"])
            p.add("pool", lambda e: e.memset(bones[64:128, 64:128], 1.0), reads=["bones"], writes=["bones"])
            p.dma("sp", lambda e: e.dma_start(out=scol, in_=cT_d), write="scol")
            p.add("act", lambda e: e.activation(out=scolb, in_=scol, func=AF.Silu), reads=["scol"], writes=["scolb"])
            p.add("pool", lambda e: e.memset(sc12, 0.0), writes=["sc12"])
            for bb in range(12):
                p.add("dve", lambda e, bb=bb: e.tensor_copy(out=sc12[:, :, bb, bb], in_=scolb), reads=["scolb", "sc12"], writes=[("sc12c", bb)])
            sck = [("sc12c", bb) for bb in range(12)]
            brow12 = b.alloc([12, 512], F32)
            p.dma("sp", lambda e: e.dma_start(out=brow12, in_=modb_d[0].rearrange("(a b) -> a b", b=512)), write="brow12")
            wkeys = RR(range(4))
            for blk in range(12):
                ws = wkeys.next()
                p.dma("pool", lambda e, blk=blk, ws=ws: e.dma_start(
                    out=wsl[ws], in_=modw_d[0, :, blk * 512:(blk + 1) * 512].rearrange("(k p) n -> p k n", p=128)),
                    write=("wsl", ws))
                def mm(e, blk=blk, ws=ws):
                    ins = None
                    for k in range(KC):
                        ins = e.matmul(PSf(7, [12, 512]), lhsT=sc12[:, k, blk, :], rhs=wsl[ws][:, k, :],
                                       start=(blk == 0 and k == 0), stop=(blk == 11 and k == KC - 1))
                    return ins
                p.add("pe", mm, reads=sck + [("wsl", ws)], writes=[("ps", 7)])
            p.add("dve", lambda e: e.tensor_tensor(out=brow12, in0=PSf(7, [12, 512]), in1=brow12, op=ALU.add),
                  reads=[("ps", 7), "brow12"], writes=["brow12"])
            p.dma("sp", lambda e: e.dma_start(out=modv_d[0].rearrange("(a b) -> a b", b=512), in_=brow12), reads=["brow12"], write=("modv", 0))
            p.finish()

        pc_state = {}

        def precast(p, l, part):
            e0 = NE - PC_EXPERTS[l]
            mats = [(e_, i_) for e_ in range(e0, NE) for i_ in range(3)]
            start = sum(PC_SPLIT[l][:part])
            for n_, (e_, i_) in enumerate(mats[start:start + PC_SPLIT[l][part]]):
                wsrc = (wg_d, wu_d, wd_d)[i_]
                p.dma("pool", lambda e, e_=e_, i_=i_, wsrc=wsrc: e.dma_start(out=wbf_d[e_ * 3 + i_], in_=wsrc[l, e_]),
                      write=("pc", n_ % 4))

        def load_modvec(p, l, which):
            base = 0 if which == 0 else 3 * D
            p.dma("sp", lambda e: e.dma_start(out=mvB, in_=modv_d[l:l + 1, base:base + D].partition_broadcast(128)[:, 0, :]), write="mvB")
            p.dma("sp", lambda e: e.dma_start(out=mvA, in_=modv_d[l:l + 1, base + D:base + 2 * D].partition_broadcast(128)[:, 0, :]), write="mvA")
            p.dma("sp", lambda e: e.dma_start(out=mvC, in_=ng_d[2 * l + which:2 * l + which + 1, :].partition_broadcast(128)[:, 0, :]), write="mvC")
            p.add("dve", lambda e: e.scalar_tensor_tensor(out=mvA, in0=mvA, scalar=1.0, in1=mvC, op0=ALU.add, op1=ALU.mult),
                  reads=["mvA", "mvC"], writes=["mvA"])

        def load_gate(p, l, which):
            base = 2 * D if which == 0 else 5 * D
            p.dma("sp", lambda e: e.dma_start(out=mvC, in_=modv_d[l:l + 1, base:base + D].partition_broadcast(128)[:, 0, :]), write="mvC")

        def norm_stats(p, b):
            ss = b.alloc([128, NT], F32)
            rstd = b.alloc([128, NT], F32)
            junk = b.alloc([128, D], BF16)
            p.add("dve", lambda e: e.memset(ss, 0.0), writes=["ss"])
            for t in range(NT):
                p.add("act", lambda e, t=t: e.activation(out=junk, in_=xs[:, t, :], func=AF.Square, accum_out=ss[:, t:t + 1]),
                      reads=["ss"], writes=[("ssc", t)])
            p.add("act", lambda e: e.activation(out=rstd, in_=ss, func=AF.Sqrt, bias=epst, scale=1.0 / D),
                  reads=[("ssc", t) for t in range(NT)] + ["ss"], writes=["rstd"])
            p.add("dve", lambda e: e.reciprocal(out=rstd, in_=rstd), reads=["rstd"], writes=["rstd"])
            return rstd

        def norm_chunk(p, t, rstd, tmp, tmpk, dst, dstk):
            o = p.add("dve", lambda e: e.scalar_tensor_tensor(out=tmp, in0=xs[:, t, :], scalar=rstd[:, t:t + 1], in1=mvA,
                                                              op0=ALU.mult, op1=ALU.mult),
                      reads=["rstd", "mvA"], writes=[tmpk])
            p.add("dve", lambda e: e.tensor_tensor(out=dst, in0=tmp, in1=mvB, op=ALU.add),
                  reads=[tmpk, "mvB"], writes=[dstk])
            return o

        def out_proj(p, l, wo_src, lhs_of, b, after_load=None):
            wo = b.alloc([128, KC, D], BF16)
            p.dma("pool", lambda e: e.dma_start(out=wo, in_=wo_src.rearrange("(k p) n -> p k n", p=128)), write="wo")
            if after_load is not None:
                after_load()
            load_gate(p, l, 0)
            for k in range(KC):
                eng = "dve"
                p.add(eng, lambda e, k=k: e.tensor_tensor(out=wo[:, k, :], in0=wo[:, k, :], in1=mvC, op=ALU.mult),
                      reads=["wo", "mvC"], writes=[("wos", k)])
            pk = RR(range(8))
            for t in range(NT):
                for hf in range(2):
                    pb = pk.next()
                    _mm(p, PSf(pb), [(lhs_of(k, t), wo[:, k, hf * 512:(hf + 1) * 512]) for k in range(KC)],
                        reads=["catT"] + [("wos", k) for k in range(KC)], writes=[("ps", pb)])
                    p.add("dve", lambda e, t=t, hf=hf, pb=pb: e.tensor_tensor(
                        out=xs[:, t, hf * 512:(hf + 1) * 512], in0=xs[:, t, hf * 512:(hf + 1) * 512], in1=PSf(pb), op=ALU.add),
                        reads=[("ps", pb)], writes=[("xs", t, hf)])

        def phase_mix_even(l):
            j = l // 2
            K = 1024
            convT = AR.view(PH0, [128, 4, S], BF16)
            hT = AR.view(PH0 + 16 * K, [128, KC, S], BF16)
            qT = AR.view(PH0 + 48 * K, [128, 4, S], BF16)
            kT = AR.view(PH0 + 64 * K, [128, 4, S], BF16)
            vaug = AR.view(PH0 + 80 * K, [128, NT, 8, 65], BF16)
            p = Prog(nc, ctx)
            b = Bump(AR, PH0 + 48 * K, PHN)
            load_modvec(p, l, 0)
            rstd = norm_stats(p, b)
            tmps = [b.alloc([128, D], F32) for _ in range(2)]
            hcs = [b.alloc([128, D], BF16) for _ in range(2)]
            wsl = [b.alloc([128, KC, 512], BF16) for _ in range(3)]
            cwt = b.alloc([128, 12], F32)
            qkg = b.alloc([128, 2], F32)
            us = [b.alloc([128, S + 2], BF16) for _ in range(2)]
            bgs = [b.alloc([128, S], BF16) for _ in range(2)]
            cgs = [b.alloc([128, 512], F32) for _ in range(2)]
            zs = [b.alloc([128, 512], F32) for _ in range(2)]
            z1s = [b.alloc([128, 512], F32) for _ in range(2)]
            p.dma("sp", lambda e: e.dma_start(out=cwt, in_=cw_d[j]), write="cwt")
            for i, sec in enumerate((3, 4, 5)):
                p.dma("pool", lambda e, i=i, sec=sec: e.dma_start(
                    out=wsl[i], in_=win_d[j, :, sec * 512:(sec + 1) * 512].rearrange("(k p) n -> p k n", p=128)), write=("wsl", i))
            for i in range(2):
                p.add("pool", lambda e, i=i: e.memset(us[i][:, 0:1], 0.0), writes=[("upad", i)])
                p.add("pool", lambda e, i=i: e.memset(us[i][:, S + 1:S + 2], 0.0), writes=[("upad2", i)])
            precast(p, l, 0)
            for t in range(NT):
                s2 = t % 2
                norm_chunk(p, t, rstd, tmps[s2], ("tmp", s2), hcs[s2], ("hc", s2))
                for kh in range(2):
                    pb = (t * 2 + kh) % 2
                    def tr(e, s2=s2, kh=kh, pb=pb):
                        ins = None
                        for jj in range(4):
                            k = kh * 4 + jj
                            ins = e.transpose(out=PSb(pb, [128, 4, 128])[:, jj, :], in_=hcs[s2][:, k * 128:(k + 1) * 128], identity=ident)
                        return ins
                    p.add("pe", tr, reads=[("hc", s2), "ident"], writes=[("ps", pb)])
                    eng = "act" if kh == 0 else "dve"
                    if eng == "act":
                        p.add("act", lambda e, t=t, kh=kh, pb=pb: e.copy(out=hT[:, kh * 4:(kh + 1) * 4, t * 128:(t + 1) * 128], in_=PSb(pb, [128, 4, 128])),
                              reads=[("ps", pb)], writes=[("hT", t)])
                    else:
                        p.add("dve", lambda e, t=t, kh=kh, pb=pb: e.tensor_copy(out=hT[:, kh * 4:(kh + 1) * 4, t * 128:(t + 1) * 128], in_=PSb(pb, [128, 4, 128])),
                              reads=[("ps", pb)], writes=[("hT", t)])
            hT_all = [("hT", t) for t in range(NT)]
            pk = RR(range(2, 8))
            for cc in range(4):
                ub = cc % 2
                u, bg = us[ub], bgs[ub]
                for n in range(4):
                    pcg, pxv, pbg = pk.next(), pk.next(), pk.next()
                    hkeys = [("hT", t) for t in range(n * 4, n * 4 + 4)]
                    cs_ = n % 2
                    _mm(p, PSf(pcg), [(wsl[1][:, k, cc * 128:(cc + 1) * 128], hT[:, k, n * 512:(n + 1) * 512]) for k in range(KC)],
                        reads=hkeys + [("wsl", 1)], writes=[("ps", pcg)])
                    p.add("act", lambda e, pcg=pcg, cs_=cs_: e.copy(out=cgs[cs_], in_=PSf(pcg)), reads=[("ps", pcg)], writes=[("cgs", cs_)])
                    _mm(p, PSf(pxv), [(wsl[2][:, k, cc * 128:(cc + 1) * 128], hT[:, k, n * 512:(n + 1) * 512]) for k in range(KC)],
                        reads=hkeys + [("wsl", 2)], writes=[("ps", pxv)])
                    p.add("dve", lambda e, pxv=pxv, cs_=cs_, u=u, n=n: e.tensor_tensor(
                        out=u[:, 1 + n * 512:1 + (n + 1) * 512], in0=PSf(pxv), in1=cgs[cs_], op=ALU.mult),
                        reads=[("ps", pxv), ("cgs", cs_)], writes=[("u", ub, n)])
                    _mm(p, PSf(pbg), [(wsl[0][:, k, cc * 128:(cc + 1) * 128], hT[:, k, n * 512:(n + 1) * 512]) for k in range(KC)],
                        reads=hkeys + [("wsl", 0)], writes=[("ps", pbg)])
                    p.add("act", lambda e, pbg=pbg, bg=bg, n=n: e.copy(out=bg[:, n * 512:(n + 1) * 512], in_=PSf(pbg)),
                          reads=[("ps", pbg)], writes=[("bg", ub, n)])
                ukeys = [("u", ub, n) for n in range(4)] + [("upad", ub), ("upad2", ub)]
                for n in range(4):
                    zi = n % 2
                    z = zs[zi]
                    z1 = z1s[zi]
                    p.add("dve", lambda e, z=z, u=u, n=n, cc=cc: e.tensor_scalar(
                        out=z, in0=u[:, n * 512:n * 512 + 512], scalar1=cwt[:, cc * 3 + 0:cc * 3 + 1], scalar2=None, op0=ALU.mult),
                        reads=ukeys + ["cwt"], writes=[("z", zi)])
                    p.add("dve", lambda e, z=z, u=u, n=n, cc=cc: e.scalar_tensor_tensor(
                        out=z, in0=u[:, 1 + n * 512:1 + n * 512 + 512], scalar=cwt[:, cc * 3 + 1:cc * 3 + 2], in1=z, op0=ALU.mult, op1=ALU.add),
                        reads=ukeys + [("z", zi), "cwt"], writes=[("z", zi)])
                    p.add("dve", lambda e, z=z, u=u, n=n, cc=cc: e.scalar_tensor_tensor(
                        out=z, in0=u[:, 2 + n * 512:2 + n * 512 + 512], scalar=cwt[:, cc * 3 + 2:cc * 3 + 3], in1=z, op0=ALU.mult, op1=ALU.add),
                        reads=ukeys + [("z", zi), "cwt"], writes=[("z", zi)])
                    p.add("dve", lambda e, z=z, bg=bg, n=n, cc=cc: e.tensor_tensor(
                        out=convT[:, cc, n * 512:(n + 1) * 512], in0=z, in1=bg[:, n * 512:(n + 1) * 512], op=ALU.mult),
                        reads=[("z", zi), ("bg", ub, n)], writes=[("convT", cc, n)])
            p.finish()

            p = Prog(nc, ctx)
            b = Bump(AR, PH0 + 97 * K, PHN)
            wsl2 = [b.alloc([128, KC, 512], BF16) for _ in range(2)]
            wsl = [wsl2[0], wsl2[1], wsl2[0]]
            qkg = b.alloc([128, 2], F32)
            qkg2 = b.alloc([128, 2], F32)
            sqs = [b.alloc([128, 512], BF16) for _ in range(2)]
            rrs = [b.alloc([128, 512], F32) for _ in range(2)]
            for i, sec in enumerate((0, 1, 2)):
                if i == 2:
                    continue
                p.dma("pool", lambda e, i=i, sec=sec: e.dma_start(
                    out=wsl[i], in_=win_d[j, :, sec * 512:(sec + 1) * 512].rearrange("(k p) n -> p k n", p=128)), write=("wsl", i % 2))
            p.dma("sp", lambda e: e.dma_start(out=qkg, in_=qkg_d[j]), write="qkg")
            precast(p, l, 3)
            p.add("dve", lambda e: e.tensor_scalar(out=qkg2[:, 0:1], in0=qkg[:, 0:1], scalar1=0.125, scalar2=None, op0=ALU.mult),
                  reads=["qkg"], writes=["qkg2a"])
            p.add("dve", lambda e: e.tensor_scalar(out=qkg2[:, 1:2], in0=qkg[:, 1:2], scalar1=1.0, scalar2=None, op0=ALU.mult),
                  reads=["qkg"], writes=["qkg2b"])
            p.add("pool", lambda e: e.memset(vaug[:, :, :, 64:65], 1.0), writes=["vones"])
            pk = RR(range(0, 6))
            pk2 = RR(range(6, 8))
            tiles = [(si, dst, pc, n) for si, dst in ((0, qT), (1, kT)) for pc in range(4) for n in range(4)]
            pqs = [pk.next() for _ in tiles]

            def qk_front(i):
                si, dst, pc, n = tiles[i]
                pq, s2 = pqs[i], i % 2
                _mm(p, PSf(pq), [(wsl[si][:, k, pc * 128:(pc + 1) * 128], hT[:, k, n * 512:(n + 1) * 512]) for k in range(KC)],
                    reads=[("wsl", si % 2)], writes=[("ps", pq)])
                p.add("act", lambda e: e.activation(out=sqs[s2], in_=PSf(pq), func=AF.Square),
                      reads=[("ps", pq)], writes=[("sq", s2)])

            def qk_back(i):
                si, dst, pc, n = tiles[i]
                pq, s2 = pqs[i], i % 2
                pss = pk2.next()
                _mm(p, PSf(pss), [(bones, sqs[s2])], reads=[("sq", s2)], writes=[("ps", pss)])
                p.add("act", lambda e: e.activation(out=rrs[s2], in_=PSf(pss), func=AF.Ln, bias=epst, scale=1.0 / 64),
                      reads=[("ps", pss)], writes=[("rr", s2)])
                p.add("act", lambda e: e.activation(out=rrs[s2], in_=rrs[s2], func=AF.Exp, scale=-0.5),
                      reads=[("rr", s2)], writes=[("rr", s2)])
                p.add("dve", lambda e: e.scalar_tensor_tensor(
                    out=dst[:, pc, n * 512:(n + 1) * 512], in0=PSf(pq), scalar=qkg2[:, si:si + 1], in1=rrs[s2],
                    op0=ALU.mult, op1=ALU.mult),
                    reads=[("ps", pq), ("rr", s2), "qkg2a", "qkg2b"], writes=[("qk", si, pc, n)])

            for i in range(len(tiles)):
                qk_front(i)
                if i >= 1:
                    qk_back(i - 1)
            qk_back(len(tiles) - 1)
            p.dma("pool", lambda e: e.dma_start(
                out=wsl[2], in_=win_d[j, :, 2 * 512:3 * 512].rearrange("(k p) n -> p k n", p=128)), write=("wsl", 0))
            for t in range(NT):
                pv = pk.next()
                _mm(p, PSf(pv), [(hT[:, k, t * 128:(t + 1) * 128], wsl[2][:, k, :]) for k in range(KC)],
                    reads=[("wsl", 0)], writes=[("ps", pv)])
                p.add("act", lambda e, t=t, pv=pv: e.copy(out=vaug[:, t, :, 0:64], in_=PSf(pv, [128, 8, 64])),
                      reads=[("ps", pv)], writes=[("v", t)])
            p.finish()

            p = Prog(nc, ctx)
            b = Bump(AR, PH0 + 16 * K, PH0 + 48 * K)
            attn_tok = b.alloc([128, NT, 512], BF16)
            tmpb = [b.alloc([128, 5, 128], F32) for _ in range(3)]
            ebs = [b.alloc([128, 5, 128], BF16) for _ in range(3)]
            recs = [b.alloc([128, 1], F32) for _ in range(4)]
            b2 = Bump(AR, PH0 + 97 * K, PHN)
            bias = [b2.alloc([128, 5, 5, 128], F32) for _ in range(2)]
            precast(p, l, 1)
            def att_front(h, t, it):
                pc, hf = h // 2, h % 2
                bs = h % 2
                c0, nk = _att_chunks(t)
                cls = _att_cls(t)
                s2 = it % 3
                psA, psB = 2 * s2, 2 * s2 + 1
                def qk(e):
                    ins = None
                    for jj in range(nk):
                        outp = PSf(psA, [128, 4, 128])[:, jj, :] if jj < 4 else PSf(psB, [128, 4, 128])[:, 0, :]
                        ins = e.matmul(outp, lhsT=kT[hf * 64:(hf + 1) * 64, pc, (c0 + jj) * 128:(c0 + jj + 1) * 128],
                                       rhs=qT[hf * 64:(hf + 1) * 64, pc, t * 128:(t + 1) * 128], start=True, stop=True)
                    return ins
                p.add("pe", qk, reads=[], writes=[("ps", psA), ("ps", psB)])
                n4 = min(nk, 4)
                p.add("dve", lambda e: e.tensor_tensor(
                    out=tmpb[s2][:, 0:n4, :], in0=PSf(psA, [128, 4, 128])[:, 0:n4, :], in1=bias[bs][:, cls, 0:n4, :], op=ALU.add),
                    reads=[("ps", psA), ("bias", bs)], writes=[("tmpb", s2)])
                if nk == 5:
                    p.add("dve", lambda e: e.tensor_tensor(
                        out=tmpb[s2][:, 4, :], in0=PSf(psB, [128, 4, 128])[:, 0, :], in1=bias[bs][:, cls, 4, :], op=ALU.add),
                        reads=[("ps", psB), ("bias", bs), ("tmpb", s2)], writes=[("tmpb", s2)])
                p.add("act", lambda e: e.activation(out=ebs[s2][:, 0:nk, :], in_=tmpb[s2][:, 0:nk, :], func=AF.Exp),
                      reads=[("tmpb", s2)], writes=[("eb", s2)])

            def att_back(h, t, it):
                c0, nk = _att_chunks(t)
                s2 = it % 3
                pso = 6 + (it % 2)
                rc = it % 4
                def pv(e):
                    ins = None
                    for jj in range(nk):
                        ins = e.matmul(PSf(pso, [128, 65]), lhsT=ebs[s2][:, jj, :], rhs=vaug[:, c0 + jj, h, :],
                                       start=(jj == 0), stop=(jj == nk - 1))
                    return ins
                p.add("pe", pv, reads=[("eb", s2)], writes=[("ps", pso)])
                p.add("dve", lambda e: e.reciprocal(out=recs[rc], in_=PSf(pso, [128, 65])[:, 64:65]),
                      reads=[("ps", pso)], writes=[("rec", rc)])
                p.add("act", lambda e: e.activation(
                    out=attn_tok[:, t, h * 64:(h + 1) * 64], in_=PSf(pso, [128, 65])[:, 0:64], func=AF.Copy, scale=recs[rc]),
                    reads=[("ps", pso), ("rec", rc)], writes=[("attn", t)])

            seq = [(h, t) for h in range(8) for t in range(NT)]
            for i, (h, t) in enumerate(seq):
                if t == 0:
                    bs = h % 2
                    p.dma("sp", lambda e, h=h, bs=bs: e.dma_start(out=bias[bs], in_=btab_d[j, h].rearrange("p (a b c) -> p a b c", a=5, b=5)),
                          write=("bias", bs))
                att_front(h, t, i)
                if i >= 2:
                    att_back(seq[i - 2][0], seq[i - 2][1], i - 2)
            for i in (len(seq) - 2, len(seq) - 1):
                att_back(seq[i][0], seq[i][1], i)
            p.finish()

            p = Prog(nc, ctx)
            attnT = AR.view(PH0 + 48 * K, [128, 4, S], BF16)
            b = Bump(AR, PH0 + 64 * K, PHN)
            for t in range(NT):
                pb = t % 2
                def tr(e, t=t, pb=pb):
                    ins = None
                    for jj in range(4):
                        ins = e.transpose(out=PSb(pb, [128, 4, 128])[:, jj, :], in_=attn_tok[:, t, jj * 128:(jj + 1) * 128], identity=ident)
                    return ins
                p.add("pe", tr, reads=[], writes=[("ps", pb)])
                if t % 2 == 0:
                    p.add("act", lambda e, t=t, pb=pb: e.copy(out=attnT[:, :, t * 128:(t + 1) * 128], in_=PSb(pb, [128, 4, 128])),
                          reads=[("ps", pb)], writes=["catT"])
                else:
                    p.add("dve", lambda e, t=t, pb=pb: e.tensor_copy(out=attnT[:, :, t * 128:(t + 1) * 128], in_=PSb(pb, [128, 4, 128])),
                          reads=[("ps", pb)], writes=["catT"])

            def lhs_of(k, t):
                return attnT[:, k, t * 128:(t + 1) * 128] if k < 4 else convT[:, k - 4, t * 128:(t + 1) * 128]
            out_proj(p, l, wout_d[j], lhs_of, b, after_load=lambda: precast(p, l, 2))
            p.finish()

        def phase_mix_odd(l):
            j = l // 2
            K = 1024
            htok = AR.view(PH0, [128, NT, D], BF16)
            fT = AR.view(PH0, [128, KC, S], BF16)
            GrT = AR.view(PH0 + 32 * K, [128, KC, S], BF16)
            GsT = AR.view(PH0 + 64 * K, [128, KC, S], BF16)
            p = Prog(nc, ctx)
            b = Bump(AR, PH0 + 32 * K, PH0 + 64 * K)
            load_modvec(p, l, 0)
            rstd = norm_stats(p, b)
            tmps = [b.alloc([128, D], F32) for _ in range(2)]
            for t in range(NT):
                s2 = t % 2
                norm_chunk(p, t, rstd, tmps[s2], ("tmp", s2), htok[:, t, :], ("htok", t))
            b = Bump(AR, PH0 + 96 * K, PHN)
            NSL = 3
            dsl = [(b.alloc([128, 4, 512], BF16), b.alloc([128, 4, 512], BF16)) for _ in range(NSL)]
            nyq = b.alloc([128, NT], BF16)
            precast(p, l, 0)
            p.dma("sp", lambda e: e.dma_start(out=nyq, in_=nyq_d), write="nyq")
            si = 0
            for half in range(2):
                for n in range(2):
                    for kg in range(4):
                        sl = si % NSL
                        si += 1
                        p.dma("sp", lambda e, sl=sl, kg=kg, n=n: e.dma_start(
                            out=dsl[sl][0], in_=dcos_d[kg * 512:(kg + 1) * 512, n * 512:(n + 1) * 512].rearrange("(k p) s -> p k s", p=128)),
                            write=("dc", sl))
                        p.dma("sp", lambda e, sl=sl, kg=kg, n=n: e.dma_start(
                            out=dsl[sl][1], in_=dsin_d[kg * 512:(kg + 1) * 512, n * 512:(n + 1) * 512].rearrange("(k p) s -> p k s", p=128)),
                            write=("ds", sl))
                        def mm(e, sl=sl, kg=kg, half=half):
                            ins = None
                            for kk in range(4):
                                k = kg * 4 + kk
                                for jj in range(4):
                                    dch = half * 4 + jj
                                    lhs = htok[:, k, dch * 128:(dch + 1) * 128]
                                    e.matmul(PSf(jj), lhsT=lhs, rhs=dsl[sl][0][:, kk, :], start=(k == 0), stop=(k == NT - 1))
                                    ins = e.matmul(PSf(4 + jj), lhsT=lhs, rhs=dsl[sl][1][:, kk, :], start=(k == 0), stop=(k == NT - 1))
                            return ins
                        p.add("pe", mm, reads=[("dc", sl), ("ds", sl)] + [("htok", kg * 4 + kk) for kk in range(4)]
                              + ([("evac", i) for i in range(8)] if kg == 0 else []),
                              writes=[("acc",)] if kg < 3 else [("ps", i) for i in range(8)])
                    for jj in range(4):
                        dch = half * 4 + jj
                        p.add("act", lambda e, jj=jj, dch=dch, n=n: e.copy(out=GrT[:, dch, n * 512:(n + 1) * 512], in_=PSf(jj)),
                              reads=[("ps", jj)], writes=[("evac", jj), "GT"])
                        p.add("dve", lambda e, jj=jj, dch=dch, n=n: e.tensor_copy(out=GsT[:, dch, n * 512:(n + 1) * 512], in_=PSf(4 + jj)),
                              reads=[("ps", 4 + jj)], writes=[("evac", 4 + jj), "GT2"])
            for dch in range(KC):
                _mm(p, PSf(0, [128, KC])[:, dch:dch + 1], [(htok[:, k, dch * 128:(dch + 1) * 128], nyq[:, k:k + 1]) for k in range(NT)],
                    reads=["nyq"] + [("htok", t) for t in range(NT)], writes=[("ps", 0)])
            p.add("act", lambda e: e.copy(out=GrT[:, :, 1024:1025], in_=PSf(0, [128, KC]).unsqueeze(2)), reads=[("ps", 0)], writes=["GTn"])
            p.add("dve", lambda e: e.memset(GsT[:, :, 1024:1025], 0.0), writes=["GT2n"])
            p.add("act", lambda e: e.copy(out=GrT[:, :, 1025:2048], in_=GrT[:, :, 1023:0:-1]), reads=["GT"], writes=["GTm"])
            p.add("dve", lambda e: e.tensor_scalar(out=GsT[:, :, 1025:2048], in0=GsT[:, :, 1023:0:-1], scalar1=-1.0, scalar2=None, op0=ALU.mult),
                  reads=["GT2"], writes=["GT2m"])
            p.finish()
            p = Prog(nc, ctx)
            b = Bump(AR, PH0 + 96 * K, PHN)
            cct = b.alloc([128, 2, 256], BF16)
            cst = b.alloc([128, 2, 256], BF16)
            p.dma("sp", lambda e: e.dma_start(out=cct, in_=cc_d.rearrange("(k p) n -> p k n", p=128)), write="cct")
            p.dma("sp", lambda e: e.dma_start(out=cst, in_=cs_d.rearrange("(k p) n -> p k n", p=128)), write="cst")
            pk = RR(range(8))
            idx = 0
            for g in range(4):
                for j2 in range(2):
                    for n in range(4):
                        pb = pk.next()
                        pairs = []
                        for i2 in range(2):
                            pairs.append((cct[:, i2, j2 * 128:(j2 + 1) * 128], GrT[:, 2 * g + i2, n * 512:(n + 1) * 512]))
                            pairs.append((cst[:, i2, j2 * 128:(j2 + 1) * 128], GsT[:, 2 * g + i2, n * 512:(n + 1) * 512]))
                        _mm(p, PSf(pb), pairs, reads=["cct", "cst"], writes=[("ps", pb)])
                        if idx % 2 == 0:
                            p.add("act", lambda e, pb=pb, g=g, j2=j2, n=n: e.copy(out=fT[:, 2 * g + j2, n * 512:(n + 1) * 512], in_=PSf(pb)),
                                  reads=[("ps", pb)], writes=["catT"])
                        else:
                            p.add("dve", lambda e, pb=pb, g=g, j2=j2, n=n: e.tensor_copy(out=fT[:, 2 * g + j2, n * 512:(n + 1) * 512], in_=PSf(pb)),
                                  reads=[("ps", pb)], writes=["catT"])
                        idx += 1
            b3 = Bump(AR, PH0 + 32 * K, PH0 + 96 * K)
            b3 = Bump(AR, PH0 + 98 * K, PHN)
            out_proj(p, l, fw_d[j], lambda k, t: fT[:, k, t * 128:(t + 1) * 128], b3, after_load=lambda: precast(p, l, 1))
            p.finish()

        def phase_moe(l, last):
            K = 1024
            p = Prog(nc, ctx)
            NSLOT = 10
            wb = Bump(AR, PH0, PH0 + 80 * K)
            wsl = [wb.alloc([128, KC, 512], BF16) for _ in range(NSLOT)]
            eb = Bump(AR, PH0 + 80 * K, PHN)
            xes = [eb.alloc([128, 2, D], BF16) for _ in range(3)]
            xeTs = [eb.alloc([128, KC, CAP], BF16) for _ in range(2)]
            hids = [eb.alloc([128, KC, CAP], BF16) for _ in range(1)] * 2
            ysbs = [eb.alloc([128, 2, D], F32) for _ in range(2)]
            e_end = eb.cur
            sas = [eb.alloc([128, CAP], F32) for _ in range(2)]
            idx_tok = eb.alloc([128, 2, NE], I32)
            gate_tok = eb.alloc([128, 2, NE], F32)
            rwt = eb.alloc([128, KC, NE], BF16)
            r1 = Bump(AR, PH0 + 80 * K, PH0 + 104 * K)
            r2 = Bump(AR, PH0 + 112 * K, PH0 + 120 * K)
            affT = r1.alloc([NE, S], F32)
            tmps = [r1.alloc([128, D], F32) for _ in range(1)] * 2
            hcs = [r1.alloc([128, D], BF16) for _ in range(4)]
            hTc = [r1.alloc([128, KC, 128], BF16) for _ in range(2)]
            vals = r2.alloc([NE, CAP], F32)
            idxu = r2.alloc([NE, CAP], U32)
            idxf = r2.alloc([NE, CAP], F32)
            ss = r2.alloc([128, NT], F32)
            rstd = r2.alloc([128, NT], F32)
            junk = r2.alloc([128, D], BF16)
            lgs = r2.alloc([128, NT, NE], F32)
            mask_hi = r2.alloc([NE, 1], I32)
            mask_lo = r2.alloc([NE, 1], I32)
            mxall = r2.alloc([128, NT], F32)
            small = r2.alloc([128, NT], F32)

            has_mod = (l + 1 < DEPTH) and not NO_MOD_RING
            srcs = []
            if has_mod:
                for blk in range(12):
                    srcs.append(modw_d[l + 1, :, blk * 512:(blk + 1) * 512])
            EB = len(srcs)
            for e_ in range(NE):
                for i_, wsrc in enumerate((wg_d, wu_d, wd_d)):
                    for hh in range(2):
                        if e_ >= NE - PC_EXPERTS[l]:
                            srcs.append(wbf_d[e_ * 3 + i_][hh * 512:(hh + 1) * 512, :])
                        else:
                            srcs.append(wsrc[l, e_][hh * 512:(hh + 1) * 512, :])
            NXS = 8
            wsl_all = list(wsl) + [AR.view(i * 8 * K, [128, KC, 512], BF16) for i in range(NXS)]
            wslr = [w_.rearrange("p k n -> p (k n)").rearrange("p (k n) -> p k n", n=D) for w_ in wsl_all]
            NTOT = NSLOT + NXS
            slot_of, prev_occ = [], []
            free_arena = [(s_, None) for s_ in range(NSLOT)]
            xs_free = list(range(NSLOT, NTOT))
            k_ = 0
            for m in range(len(srcs)):
                while not free_arena and (m < EB or not xs_free):
                    if slot_of[k_] < NSLOT:
                        free_arena.append((slot_of[k_], k_))
                    k_ += 1
                if free_arena:
                    s_, pv_ = free_arena.pop(0)
                else:
                    s_, pv_ = xs_free.pop(0), None
                slot_of.append(s_)
                prev_occ.append(pv_)
            xs_open = [False]
            xs_gate = []
            xs_first = set()
            w_issued = [0]
            done_upto = [-1]

            def pump():
                while w_issued[0] < len(srcs):
                    m = w_issued[0]
                    s_ = slot_of[m]
                    if prev_occ[m] is not None and prev_occ[m] > done_upto[0]:
                        break
                    after = ()
                    if s_ >= NSLOT:
                        if not xs_open[0]:
                            break
                        if s_ not in xs_first:
                            xs_first.add(s_)
                            after = tuple(xs_gate)
                    dst = wsl_all[s_] if m < EB else wslr[s_]
                    p.dma("pool", lambda e, m=m, dst=dst: e.dma_start(out=dst, in_=srcs[m].rearrange("(k p) n -> p k n", p=128)),
                          write=("w", s_), after=after)
                    w_issued[0] += 1

            def consumed(m):
                done_upto[0] = max(done_upto[0], m)
                pump()

            p.dma("pool", lambda e: e.dma_start(out=rwt, in_=rw_d[l].rearrange("(k p) n -> p k n", p=128)), write="rwt")
            pump()
            if has_mod:
                brow12 = eb.alloc([12, 512], F32)
                p.dma("sp", lambda e: e.dma_start(out=brow12, in_=modb_d[l + 1].rearrange("(a b) -> a b", b=512)), write="brow12")

            def mod_block(blk):
                sl = slot_of[blk]
                def mm(e):
                    ins = None
                    for k in range(KC):
                        ins = e.matmul(PSf(7, [12, 512]), lhsT=sc12[:, k, blk, :], rhs=wsl_all[sl][:, k, :],
                                       start=(blk == 0 and k == 0), stop=(blk == 11 and k == KC - 1))
                    return ins
                p.add("pe", mm, reads=[("w", sl)], writes=[("ps", 7)])
                consumed(blk)
                if blk == 11:
                    p.add("dve", lambda e: e.tensor_tensor(out=brow12, in0=PSf(7, [12, 512]), in1=brow12, op=ALU.add),
                          reads=[("ps", 7), "brow12"], writes=["brow12"])
                    p.dma("sp", lambda e: e.dma_start(out=modv_d[l + 1].rearrange("(a b) -> a b", b=512), in_=brow12),
                          reads=["brow12"], write=("modv", l + 1))
            load_modvec(p, l, 1)
            acc_vs = [a_.rearrange("(t p) d -> p t d", p=128) for a_ in acc_ds]
            load_gate(p, l, 1)
            p.add("dve", lambda e: e.memset(ss, 0.0), writes=["ss"])
            for t in range(NT):
                p.add("act", lambda e, t=t: e.activation(out=junk, in_=xs[:, t, :], func=AF.Square, accum_out=ss[:, t:t + 1]),
                      reads=["ss"], writes=[("ssc", t)])
            p.add("act", lambda e: e.activation(out=rstd, in_=ss, func=AF.Sqrt, bias=epst, scale=1.0 / D),
                  reads=[("ssc", t) for t in range(NT)], writes=["rstd"])
            p.add("dve", lambda e: e.reciprocal(out=rstd, in_=rstd), reads=["rstd"], writes=["rstd"])
            for t in range(NT):
                s2 = t % 2
                s4 = t % 4
                o_n = norm_chunk(p, t, rstd, tmps[0], ("tmp", 0), hcs[s4], ("hc", s4))
                if t == NT - 1:
                    xs_gate.append(o_n)
                p.dma("sp", lambda e, t=t, s4=s4: e.dma_start(out=hdr_d[t * 128:(t + 1) * 128, :], in_=hcs[s4]),
                      reads=[("hc", s4)], write=("hdr", t % 8))
                for kh in range(2):
                    pb = kh
                    def tr(e, s4=s4, kh=kh, pb=pb):
                        ins = None
                        for jj in range(4):
                            k = kh * 4 + jj
                            ins = e.transpose(out=PSb(pb, [128, 4, 128])[:, jj, :], in_=hcs[s4][:, k * 128:(k + 1) * 128], identity=ident)
                        return ins
                    p.add("pe", tr, reads=[("hc", s4)], writes=[("ps", pb)])
                    if kh == 0:
                        p.add("act", lambda e, s2=s2, pb=pb: e.copy(out=hTc[s2][:, 0:4, :], in_=PSb(pb, [128, 4, 128])),
                              reads=[("ps", pb)], writes=[("hTc", s2, 0)])
                    else:
                        p.add("dve", lambda e, s2=s2, pb=pb: e.tensor_copy(out=hTc[s2][:, 4:8, :], in_=PSb(pb, [128, 4, 128])),
                              reads=[("ps", pb)], writes=[("hTc", s2, 1)])
                _mm(p, PSf(2, [128, NT, NE])[:, t, :], [(hTc[s2][:, k, :], rwt[:, k, :]) for k in range(KC)],
                    reads=[("hTc", s2, 0), ("hTc", s2, 1), "rwt"], writes=[("lgps", t)])
                if has_mod and 2 <= t < 14:
                    mod_block(t - 2)
            p.add("pool", lambda e: e.memset(ysbs[0], 0.0), writes=[("ysb", 0)])
            for q in range(4):
                xs_gate.append(p.dma("sp", lambda e, q=q: e.dma_start(out=acc_vs[0][:, 4 * q:4 * q + 4, :], in_=xs[:, 4 * q:4 * q + 4, :]),
                                     write=("acc0", 0, q)))
            for q in range(8):
                p.dma("sp", lambda e, q=q: e.dma_start(out=acc_vs[1][:, 2 * q:2 * q + 2, :], in_=ysbs[0]),
                      reads=[("ysb", 0)], write=("acc0", 1, q % 4))
            lgk = [("lgps", t) for t in range(NT)]
            lg3 = PSf(2, [128, NT, NE])
            p.add("dve", lambda e: e.tensor_reduce(out=mxall, in_=lg3, axis=AX.X, op=ALU.max), reads=lgk, writes=["mxall"])
            p.add("dve", lambda e: e.tensor_tensor(out=lgs, in0=lg3, in1=mxall.unsqueeze(2).broadcast_to([128, NT, NE]), op=ALU.subtract),
                  reads=lgk + ["mxall"], writes=["lgs"])
            p.add("act", lambda e: e.activation(out=lgs, in_=lgs, func=AF.Exp), reads=["lgs"], writes=["lgs"])
            p.add("dve", lambda e: e.tensor_reduce(out=small, in_=lgs, axis=AX.X, op=ALU.add), reads=["lgs"], writes=["small"])
            p.add("dve", lambda e: e.reciprocal(out=small, in_=small), reads=["small"], writes=["small"])
            p.add("dve", lambda e: e.tensor_tensor(out=lgs, in0=lgs, in1=small.unsqueeze(2).broadcast_to([128, NT, NE]), op=ALU.mult),
                  reads=["lgs", "small"], writes=["lgs"])
            for t in range(NT):
                bk = 3 + t // 4
                p.add("pe", lambda e, t=t, bk=bk: e.transpose(out=PSf(bk, [NE, 512])[:, (t % 4) * 128:(t % 4 + 1) * 128], in_=lgs[:, t, :], identity=identf),
                      reads=["lgs"], writes=[("ps", bk)])
                if t % 4 == 3:
                    p.add("act", lambda e, t=t, bk=bk: e.copy(out=affT[:, (t // 4) * 512:(t // 4 + 1) * 512], in_=PSf(bk, [NE, 512])),
                          reads=[("ps", bk)], writes=[("affT", t // 4)])
            affk = [("affT", t) for t in range(4)]
            iot = AR.view(PH0 + 88 * K, [NE, S], I32)
            affT_i = affT.bitcast(I32)
            vals_i = vals.bitcast(I32)
            idxi = idxu.bitcast(I32)
            p.add("pool", lambda e: e.memset(mask_hi, -2048), writes=["mask_hi"])
            p.add("pool", lambda e: e.memset(mask_lo, 2047), writes=["mask_lo"])
            p.add("pool", lambda e: e.iota(iot, pattern=[[1, S]], base=0, channel_multiplier=0),
                  reads=[("tmp", 0), ("hc", 0), ("hc", 1)], writes=[("tmp", 0), ("hc", 0), ("hc", 1), "iot"])
            xs_open[0] = True
            pump()
            p.add("dve", lambda e: e.tensor_scalar(out=affT_i, in0=affT_i, scalar1=mask_hi, scalar2=None, op0=ALU.bitwise_and),
                  reads=affk + ["mask_hi"], writes=["affw"])
            p.add("dve", lambda e: e.tensor_tensor(out=affT_i, in0=affT_i, in1=iot, op=ALU.bitwise_or),
                  reads=["affw", "iot"], writes=["affw"])
            for i in range(CAP // 8):
                sl = slice(8 * i, 8 * i + 8)
                p.add("dve", lambda e, sl=sl: e.max(out=vals[:, sl], in_=affT), reads=["affw"], writes=[("vals", i)])
                p.add("dve", lambda e, sl=sl: e.match_replace(out=affT, in_to_replace=vals[:, sl], in_values=affT, imm_value=-1.0),
                      reads=[("vals", i), "affw"], writes=["affw"])
            vk = [("vals", i) for i in range(CAP // 8)]
            p.add("dve", lambda e: e.tensor_scalar(out=idxi, in0=vals_i, scalar1=mask_lo, scalar2=None, op0=ALU.bitwise_and),
                  reads=vk + ["mask_lo"], writes=["idxi"])
            p.add("dve", lambda e: e.tensor_copy(out=idxf, in_=idxi), reads=["idxi"], writes=["idxf"])
            p.add("dve", lambda e: e.tensor_scalar(out=vals_i, in0=vals_i, scalar1=mask_hi, scalar2=None, op0=ALU.bitwise_and),
                  reads=vk + ["idxi", "mask_hi"], writes=vk)
            for c2 in range(2):
                p.add("pe", lambda e, c2=c2: e.transpose(out=PSf(3 + c2, [128, NE]), in_=idxf[:, c2 * 128:(c2 + 1) * 128], identity=identf[0:NE, 0:NE]),
                      reads=["idxf"], writes=[("ps", 3 + c2)])
                p.add("dve", lambda e, c2=c2: e.tensor_copy(out=idx_tok[:, c2, :], in_=PSf(3 + c2, [128, NE])),
                      reads=[("ps", 3 + c2)], writes=[("idx_tok", c2)])
                p.add("pe", lambda e, c2=c2: e.transpose(out=PSf(5, [128, 2, NE])[:, c2, :], in_=vals[:, c2 * 128:(c2 + 1) * 128], identity=identf[0:NE, 0:NE]),
                      reads=[("vals", i) for i in range(CAP // 8)], writes=[("ps", 5)])
                p.add("act", lambda e, c2=c2: e.copy(out=gate_tok[:, c2, :], in_=PSf(5, [128, 2, NE])[:, c2, :]),
                      reads=[("ps", 5)], writes=[("gate_tok", c2)])
            if debug_out:
                dbg_i = nc.dram_tensor("dbg_idx", [128, 2 * NE], I32, kind="ExternalOutput").ap()
                dbg_g = nc.dram_tensor("dbg_gate", [128, 2 * NE], F32, kind="ExternalOutput").ap()
                dbg_a = nc.dram_tensor("dbg_vals", [NE, CAP], F32, kind="ExternalOutput").ap()
                dbg_u = nc.dram_tensor("dbg_idxu", [NE, CAP], U32, kind="ExternalOutput").ap()
                p.dma("sp", lambda e: e.dma_start(out=dbg_i, in_=idx_tok.rearrange("p a b -> p (a b)")), reads=[("idx_tok", 0), ("idx_tok", 1)], write="dbg_i")
                p.dma("sp", lambda e: e.dma_start(out=dbg_g, in_=gate_tok.rearrange("p a b -> p (a b)")), reads=[("gate_tok", 0), ("gate_tok", 1)], write="dbg_g")
                p.dma("sp", lambda e: e.dma_start(out=dbg_a, in_=vals), reads=[("vals", i) for i in range(CAP // 8)], write="dbg_a")
                p.dma("sp", lambda e: e.dma_start(out=dbg_u, in_=idxu), reads=["idxi"], write="dbg_u")
            hdr_keys = [("hdr", q) for q in range(8)]
            tk = RR([0, 1])

            def gather(ex):
                s3 = ex % 3
                for c2 in range(2):
                    p.dma("pool", lambda e, ex=ex, c2=c2, s3=s3: e.indirect_dma_start(
                        out=xes[s3][:, c2, :], out_offset=None, in_=hdr_d,
                        in_offset=bass.IndirectOffsetOnAxis(ap=idx_tok[:, c2, ex:ex + 1], axis=0)),
                        reads=hdr_keys + [("idx_tok", c2)], write=("xe", s3, c2))

            def transposes(ex):
                s2, s3 = ex % 2, ex % 3
                for c2 in range(2):
                    for kh in range(2):
                        pb = tk.next()
                        def tr(e, s3=s3, c2=c2, kh=kh, pb=pb):
                            ins = None
                            for jj in range(4):
                                k = kh * 4 + jj
                                ins = e.transpose(out=PSb(pb, [128, 4, 128])[:, jj, :], in_=xes[s3][:, c2, k * 128:(k + 1) * 128], identity=ident)
                            return ins
                        p.add("pe", tr, reads=[("xe", s3, c2)], writes=[("ps", pb)])
                        if kh == 0:
                            p.add("act", lambda e, s2=s2, c2=c2, pb=pb: e.copy(out=xeTs[s2][:, 0:4, c2 * 128:(c2 + 1) * 128], in_=PSb(pb, [128, 4, 128])),
                                  reads=[("ps", pb)], writes=[("xeT", s2, c2, 0)])
                        else:
                            p.add("dve", lambda e, s2=s2, c2=c2, pb=pb: e.tensor_copy(out=xeTs[s2][:, 4:8, c2 * 128:(c2 + 1) * 128], in_=PSb(pb, [128, 4, 128])),
                                  reads=[("ps", pb)], writes=[("xeT", s2, c2, 1)])

            gather(0)
            gather(1)
            transposes(0)
            scat = {}
            for ex in range(NE):
                s2 = ex % 2
                if ex + 2 < NE:
                    gather(ex + 2)
                xk = [("xeT", s2, c2, kh) for c2 in range(2) for kh in range(2)]
                base = EB + 6 * ex
                sl_ = [slot_of[base + i] for i in range(6)]
                for f in range(KC):
                    pa, pu = 2 + f % 2, 4 + f % 2
                    _mm(p, PSf(pa, [128, CAP]), [(wslr[sl_[k // 4]][:, k % 4, f * 128:(f + 1) * 128], xeTs[s2][:, k, :]) for k in range(KC)],
                        reads=xk + [("w", sl_[0]), ("w", sl_[1])], writes=[("ps", pa)])
                    _mm(p, PSf(pu, [128, CAP]), [(wslr[sl_[2 + k // 4]][:, k % 4, f * 128:(f + 1) * 128], xeTs[s2][:, k, :]) for k in range(KC)],
                        reads=xk + [("w", sl_[2]), ("w", sl_[3])], writes=[("ps", pu)])
                    sa = f % 2
                    p.add("act", lambda e, pa=pa, sa=sa: e.activation(out=sas[sa], in_=PSf(pa, [128, CAP]), func=AF.Silu),
                          reads=[("ps", pa)], writes=[("sa", sa)])
                    p.add("dve", lambda e, pu=pu, sa=sa, s2=s2, f=f: e.tensor_tensor(out=hids[s2][:, f, :], in0=sas[sa], in1=PSf(pu, [128, CAP]), op=ALU.mult),
                          reads=[("ps", pu), ("sa", sa)], writes=[("hid", f)])
                consumed(base + 3)
                if ex + 1 < NE:
                    transposes(ex + 1)
                hk = [("hid", f) for f in range(KC)]
                for hf in range(2):
                    for c2 in range(2):
                        py = 6 + (hf * 2 + c2) % 2
                        _mm(p, PSf(py), [(hids[s2][:, f, c2 * 128:(c2 + 1) * 128], wslr[sl_[4 + f // 4]][:, f % 4, hf * 512:(hf + 1) * 512]) for f in range(KC)],
                            reads=hk + [("w", sl_[4]), ("w", sl_[5])], writes=[("ps", py)])
                        p.add("act", lambda e, py=py, s2=s2, c2=c2, hf=hf, ex=ex: e.activation(
                            out=ysbs[s2][:, c2, hf * 512:(hf + 1) * 512], in_=PSf(py), func=AF.Copy, scale=gate_tok[:, c2, ex:ex + 1]),
                            reads=[("ps", py), ("gate_tok", c2)], writes=[("ysb", s2)])
                consumed(base + 5)
                for c2 in range(2):
                    p.add("dve", lambda e, s2=s2, c2=c2: e.tensor_tensor(out=ysbs[s2][:, c2, :], in0=ysbs[s2][:, c2, :], in1=mvC, op=ALU.mult),
                          reads=[("ysb", s2), "mvC"], writes=[("ysb", s2)])
                a_i = ex % 2
                for c2 in range(2):
                    prev_other = scat.get((ex - 2, 1 - c2))
                    scat[(ex, c2)] = p.dma("pool", lambda e, ex=ex, c2=c2, s2=s2, a_i=a_i: e.indirect_dma_start(
                        out=acc_ds[a_i], out_offset=bass.IndirectOffsetOnAxis(ap=idx_tok[:, c2, ex:ex + 1], axis=0),
                        in_=ysbs[s2][:, c2, :], in_offset=None, compute_op=ALU.add),
                        reads=[("ysb", s2), ("idx_tok", c2)] + [("acc0", a_i, q) for q in range(4)], write=("accw", a_i, c2),
                        after=[prev_other])
            acck = [("accw", a_i, c2) for a_i in range(2) for c2 in range(2)]
            cb = Bump(AR, PH0 + 80 * K, PH0 + 104 * K)
            cbufs = [(ysbs[0], ysbs[1], ("ysb", 0), ("ysb", 1)),
                     (cb.alloc([128, 2, D], F32), cb.alloc([128, 2, D], F32), ("cb", 0), ("cb", 1))]
            for q in range(8):
                bA, bB, kA, kB = cbufs[q % 2]
                p.dma("sp", lambda e, q=q, bA=bA: e.dma_start(out=bA, in_=acc_vs[0][:, 2 * q:2 * q + 2, :]), reads=acck, write=kA)
                p.dma("sp", lambda e, q=q, bB=bB: e.dma_start(out=bB, in_=acc_vs[1][:, 2 * q:2 * q + 2, :]), reads=acck, write=kB)
                p.add("dve", lambda e, q=q, bA=bA, bB=bB: e.tensor_tensor(out=xs[:, 2 * q:2 * q + 2, :], in0=bA, in1=bB, op=ALU.add),
                      reads=[kA, kB], writes=[("xs", 2 * q), ("xs", 2 * q + 1), kA, kB])
            if last:
                for q4 in range(4):
                    p.dma("sp", lambda e, q4=q4: e.dma_start(out=out_v[:, q4 * 4:(q4 + 1) * 4, :], in_=xs[:, q4 * 4:(q4 + 1) * 4, :]),
                          reads=[("xs", t) for t in range(q4 * 4, q4 * 4 + 4)], write=("out", q4))
            p.finish()

        def phase_store():
            p = Prog(nc, ctx)
            for q4 in range(4):
                p.dma("sp", lambda e, q4=q4: e.dma_start(out=out_v[:, q4 * 4:(q4 + 1) * 4, :], in_=xs[:, q4 * 4:(q4 + 1) * 4, :]), write=("out", q4))
            p.finish()

        stored = False
        for i, ph in enumerate(phases):
            if ph[0] == "mod":
                phase_mod()
            elif ph[0] == "mix":
                if ph[1] % 2 == 0:
                    phase_mix_even(ph[1])
                else:
                    phase_mix_odd(ph[1])
            elif ph[0] == "moe":
                last = (i == len(phases) - 1)
                phase_moe(ph[1], last)
                stored = stored or last
        if not stored:
            phase_store()
    return nc


_CONST = {}


def _constants():
    if not _CONST:
        s = np.arange(S, dtype=np.int64)
        ang = 2.0 * np.pi * ((s[:, None] * s[None, :]) % S).astype(np.float64) / S
        _CONST["dft_cos"] = (np.cos(ang) / np.sqrt(S)).astype(ml_dtypes.bfloat16)
        _CONST["dft_sin"] = (np.sin(ang) / np.sqrt(S)).astype(ml_dtypes.bfloat16)
        c = np.arange(256, dtype=np.int64)
        ang = 2.0 * np.pi * ((c[:, None] * c[None, :]) % 256).astype(np.float64) / 256
        _CONST["dft_cc"] = (np.cos(ang) / 16.0).astype(ml_dtypes.bfloat16)
        _CONST["dft_csn"] = (-np.sin(ang) / 16.0).astype(ml_dtypes.bfloat16)
        sgn = np.where(np.arange(S) % 2 == 0, 1.0, -1.0) / np.sqrt(S)
        _CONST["dft_nyq"] = np.ascontiguousarray(sgn.reshape(NT, 128).T).astype(ml_dtypes.bfloat16)
        _CONST["bidx"] = _bias_index()
    return _CONST


def make_in_maps(inputs):
    cst = _constants()
    f = lambda a: np.ascontiguousarray(np.asarray(a, dtype=np.float32))
    rpb = f(inputs["rpb"])
    dr, dc, ok = cst["bidx"]
    btab = np.empty((2, 8, 128, 5, 5, 128), np.float32)
    for j in range(2):
        g = rpb[j][:, dr, dc]
        g = np.where(ok[None], g, np.float32(NEG))
        btab[j] = g.transpose(0, 2, 1, 3, 4)
    btab = np.ascontiguousarray(btab.reshape(2, 8, 128, 3200))
    qk_g = f(inputs["qk_g"])
    qkg = np.ascontiguousarray(np.concatenate([qk_g, qk_g], axis=2).transpose(0, 2, 1))
    conv_w = f(inputs["conv_w"])
    cw = np.ascontiguousarray(conv_w.reshape(2, 3, 4, 128).transpose(0, 3, 2, 1).reshape(2, 128, 12))
    shared = {
        "mod_w": f(inputs["mod_w"]), "mod_b": f(inputs["mod_b"]),
        "norm_g": f(inputs["norm_g"]).reshape(DEPTH * 2, D),
        "mix_w_in": f(inputs["mix_w_in"]), "mix_w_out": f(inputs["mix_w_out"]), "fnet_w_out": f(inputs["fnet_w_out"]),
        "qkg": qkg, "cw": cw, "btab": btab, "router_w": f(inputs["router_w"]),
        "exp_w_gate": f(inputs["exp_w_gate"]), "exp_w_up": f(inputs["exp_w_up"]), "exp_w_down": f(inputs["exp_w_down"]),
        "dft_cos": cst["dft_cos"], "dft_sin": cst["dft_sin"], "dft_cc": cst["dft_cc"], "dft_csn": cst["dft_csn"], "dft_nyq": cst["dft_nyq"],
    }
    x = f(inputs["x"])
    c = f(inputs["c"])
    maps = []
    for bidx in range(x.shape[0]):
        m = dict(shared)
        m["x"] = x[bidx]
        m["cT"] = np.ascontiguousarray(c[bidx].reshape(8, 128).T)
        maps.append(m)
    return maps


def kernel(**inputs):
    nc = build_program()
    maps = make_in_maps(inputs)
    res = run_bass_kernel_spmd(nc, maps, core_ids=list(range(len(maps))))
    return np.stack([r["out"] for r in res.results], axis=0).astype(np.float32)
```

```python
from contextlib import ExitStack

import numpy as np
import ml_dtypes

import concourse.bass as bass
import concourse.mybir as mybir
from concourse.bass_utils import run_bass_kernel_spmd

F32 = mybir.dt.float32
BF16 = mybir.dt.bfloat16
I32 = mybir.dt.int32
U32 = mybir.dt.uint32
AF = mybir.ActivationFunctionType
ALU = mybir.AluOpType
AX = mybir.AxisListType

S = 2048
D = 1024
NT = 16
KC = 8
DEPTH = 4
NE = 16
CAP = 256
EPS = 1e-6
NEG = -30000.0
NO_MOD_RING = False
MERGE_POOLS = True

ENGS = ("pe", "act", "dve", "pool", "sp")


class _Op:
    __slots__ = ("eng", "emit", "deps", "signal", "dma", "sem", "val", "barrier")

    def __init__(self, eng, emit, dma=False):
        self.eng = eng
        self.emit = emit
        self.deps = []
        self.signal = False
        self.dma = dma
        self.sem = None
        self.val = 0
        self.barrier = False


class _Res:
    __slots__ = ("writer", "readers")

    def __init__(self):
        self.writer = None
        self.readers = []


class SemCtx:
    def __init__(self, nc, es, n_sw=2, n_hw=64):
        self.esem = {e: es.enter_context(nc.semaphore("s_" + e)) for e in ENGS}
        self.n = {"sw": n_sw, "hw": n_hw}
        self.dsem = {"sw": [es.enter_context(nc.semaphore("dsw%d" % i)) for i in range(n_sw)],
                     "hw": [es.enter_context(nc.semaphore("dhw%d" % i)) for i in range(n_hw)]}
        self.ecount = {e: 0 for e in ENGS}
        self.dcount = {"sw": [0] * n_sw, "hw": [0] * n_hw}


class Prog:
    def __init__(self, nc, ctx):
        self.nc = nc
        self.ctx = ctx
        self.ops = {e: [] for e in ENGS}
        self.res = {}
        self.key2sem = {}
        self.sem_last = {}

    def _collect(self, o, reads, writes):
        deps = []
        for r in reads:
            st = self.res.get(r)
            if st is not None and st.writer is not None:
                deps.append((st.writer, "raw"))
        for w in writes:
            st = self.res.get(w)
            if st is not None:
                if st.writer is not None:
                    deps.append((st.writer, "waw"))
                for rd in st.readers:
                    deps.append((rd, "war"))
        seen = set()
        for d, kind in deps:
            if d is o or id(d) in seen:
                continue
            if (not d.dma) and (not o.dma) and d.eng == o.eng:
                if kind != "raw" or o.eng == "pe":
                    continue
            seen.add(id(d))
            o.deps.append(d)
            if not d.dma:
                d.signal = True
        for r in reads:
            self.res.setdefault(r, _Res()).readers.append(o)
        for w in writes:
            st = self.res.setdefault(w, _Res())
            st.writer = o
            st.readers = []

    def add(self, eng, emit, reads=(), writes=(), after=()):
        o = _Op(eng, emit)
        self._collect(o, reads, writes)
        for d in after:
            if d is not None and all(d is not x for x in o.deps):
                o.deps.append(d)
                if not d.dma:
                    d.signal = True
        self.ops[eng].append(o)
        return o

    def dma(self, eng, emit, reads=(), write=None, after=()):
        o = _Op(eng, emit, dma=True)
        self._collect(o, reads, (write,))
        for d in after:
            if d is not None and all(d is not x for x in o.deps):
                o.deps.append(d)
                if not d.dma:
                    d.signal = True
        pool = "sw" if (eng == "pool" and not MERGE_POOLS) else "hw"
        if write not in self.key2sem:
            n_used = sum(1 for v in self.key2sem.values() if v[0] == pool)
            assert n_used < self.ctx.n[pool], "too many DMA keys in one phase (%s)" % pool
            self.key2sem[write] = (pool, n_used)
        s = self.key2sem[write]
        assert s[0] == pool, ("DMA key used from both DGE kinds", write)
        prev = self.sem_last.get(s)
        if prev is not None and all(prev is not d for d in o.deps):
            o.deps.append(prev)
        self.ctx.dcount[s[0]][s[1]] += 16
        o.sem = s
        o.val = self.ctx.dcount[s[0]][s[1]]
        assert o.val < 65000
        self.sem_last[s] = o
        self.ops[eng].append(o)
        return o

    def finish(self):
        lasts = []
        for e in ENGS:
            for o in reversed(self.ops[e]):
                if not o.dma:
                    lasts.append(o)
                    break
        dmas = list(self.sem_last.values())
        for e in ENGS:
            o = _Op(e, None)
            o.barrier = True
            for d in lasts:
                if d.eng != e:
                    o.deps.append(d)
                    d.signal = True
            o.deps.extend(dmas)
            self.ops[e].append(o)
        self._emit()

    def _emit(self):
        nc, ctx = self.nc, self.ctx
        for e in ENGS:
            c = ctx.ecount[e]
            for o in self.ops[e]:
                if o.dma or o.barrier:
                    continue
                if o.signal:
                    c += 1
                    o.val = c
            assert c < 65000
            ctx.ecount[e] = c

        def run(e, eng):
            seen = {}
            for o in self.ops[e]:
                for d in o.deps:
                    if d.dma:
                        key, sem = ("d", d.sem), ctx.dsem[d.sem[0]][d.sem[1]]
                    else:
                        key, sem = ("e", d.eng), ctx.esem[d.eng]
                    if seen.get(key, 0) >= d.val:
                        continue
                    seen[key] = d.val
                    eng.wait_ge(sem, d.val)
                if o.barrier:
                    continue
                ins = o.emit(eng)
                if o.dma:
                    ins.then_inc(ctx.dsem[o.sem[0]][o.sem[1]], 16)
                elif o.signal:
                    ins.then_inc(ctx.esem[e], 1)

        with nc.Block() as block:
            @block.tensor
            def _(eng):
                run("pe", eng)

            @block.scalar
            def _(eng):
                run("act", eng)

            @block.vector
            def _(eng):
                run("dve", eng)

            @block.gpsimd
            def _(eng):
                run("pool", eng)

            @block.sync
            def _(eng):
                run("sp", eng)


class RR:
    def __init__(self, items):
        self.items = list(items)
        self.i = 0

    def next(self):
        x = self.items[self.i % len(self.items)]
        self.i += 1
        return x


_ISZ = {F32: 4, BF16: 2, I32: 4, U32: 4}


class Arena:
    def __init__(self, ap, nbytes):
        self.ap = ap
        self.nbytes = nbytes

    def view(self, off, shape, dt):
        n = 1
        for s_ in shape[1:]:
            n *= s_
        nb = n * _ISZ[dt]
        assert off % 4 == 0 and off + nb <= self.nbytes, (off, nb, self.nbytes)
        v = self.ap[0:shape[0], off // 2:(off + nb) // 2]
        if dt != BF16:
            v = v.bitcast(dt)
        if len(shape) == 3:
            v = v.rearrange("p (a b) -> p a b", b=shape[2])
        elif len(shape) == 4:
            v = v.rearrange("p (a b c) -> p a b c", b=shape[2], c=shape[3])
        return v


class Bump:
    def __init__(self, arena, lo, hi):
        self.arena, self.cur, self.hi = arena, lo, hi

    def alloc(self, shape, dt):
        n = 1
        for s_ in shape[1:]:
            n *= s_
        nb = (n * _ISZ[dt] + 63) // 64 * 64
        off = self.cur
        assert off + nb <= self.hi, ("arena overflow", off, nb, self.hi)
        self.cur += nb
        return self.arena.view(off, shape, dt)


def _mm(p, out, pairs, reads, writes):
    def f(e):
        n = len(pairs)
        ins = None
        for i, (l, r) in enumerate(pairs):
            ins = e.matmul(out, lhsT=l, rhs=r, start=(i == 0), stop=(i == n - 1))
        return ins
    p.add("pe", f, reads, writes)


def _rs(r):
    return min(max(r - 4, 0), 24)


def _cs(c):
    return min(max(c - 8, 0), 48)


ATT_CLS_T = [0, 1, 2, 14, 15]


def _att_cls(t):
    return {0: 0, 1: 1, 14: 3, 15: 4}.get(t, 2)


def _att_chunks(t):
    c0 = (_rs(2 * t) * 64) // 128
    c1 = ((_rs(2 * t + 1) + 8) * 64 - 1) // 128
    return c0, c1 - c0 + 1


def _bias_index():
    dr = np.zeros((5, 128, 5, 128), np.int64)
    dc = np.zeros((5, 128, 5, 128), np.int64)
    ok = np.zeros((5, 128, 5, 128), bool)
    for ci, t in enumerate(ATT_CLS_T):
        c0, nk = _att_chunks(t)
        q = np.arange(128)
        qt = t * 128 + q
        r, c = qt // 64, qt % 64
        rs = np.clip(r - 4, 0, 24)
        cs = np.clip(c - 8, 0, 48)
        for j in range(nk):
            pk = np.arange(128)
            kt = (c0 + j) * 128 + pk
            kr, kc = kt // 64, kt % 64
            inside = ((kr[:, None] >= rs[None, :]) & (kr[:, None] < rs[None, :] + 8) &
                      (kc[:, None] >= cs[None, :]) & (kc[:, None] < cs[None, :] + 16))
            ok[ci, :, j, :] = inside
            dr[ci, :, j, :] = np.where(inside, kr[:, None] - r[None, :] + 7, 0)
            dc[ci, :, j, :] = np.where(inside, kc[:, None] - c[None, :] + 15, 0)
    return dr, dc, ok


def build_program(phases=None, debug_out=None):
    nc = bass.Bass("TRN2", target_bir_lowering=False)
    dt_in = lambda name, shape, dt=F32: nc.dram_tensor(name, shape, dt, kind="ExternalInput").ap()
    x_d = dt_in("x", [S, D])
    cT_d = dt_in("cT", [128, 8])
    modw_d = dt_in("mod_w", [DEPTH, D, 6 * D])
    modb_d = dt_in("mod_b", [DEPTH, 6 * D])
    ng_d = dt_in("norm_g", [DEPTH * 2, D])
    win_d = dt_in("mix_w_in", [2, D, 3 * D])
    wout_d = dt_in("mix_w_out", [2, D, D])
    fw_d = dt_in("fnet_w_out", [2, D, D])
    qkg_d = dt_in("qkg", [2, 128, 2])
    cw_d = dt_in("cw", [2, 128, 12])
    btab_d = dt_in("btab", [2, 8, 128, 3200])
    rw_d = dt_in("router_w", [DEPTH, D, NE])
    wg_d = dt_in("exp_w_gate", [DEPTH, NE, D, D])
    wu_d = dt_in("exp_w_up", [DEPTH, NE, D, D])
    wd_d = dt_in("exp_w_down", [DEPTH, NE, D, D])
    dcos_d = dt_in("dft_cos", [S, S], BF16)
    dsin_d = dt_in("dft_sin", [S, S], BF16)
    cc_d = dt_in("dft_cc", [256, 256], BF16)
    cs_d = dt_in("dft_csn", [256, 256], BF16)
    nyq_d = dt_in("dft_nyq", [128, NT], BF16)
    out_d = nc.dram_tensor("out", [S, D], F32, kind="ExternalOutput").ap()
    modv_d = nc.dram_tensor("modv", [DEPTH, 6 * D], F32, kind="Internal").ap()
    hdr_d = nc.dram_tensor("h_dram", [S, D], BF16, kind="Internal").ap()
    acc_ds = [nc.dram_tensor("acc_dram%d" % i, [S, D], F32, kind="Internal").ap() for i in range(2)]
    wbf_d = nc.dram_tensor("wbf", [NE * 3, D, D], BF16, kind="Internal").ap()
    PC_EXPERTS = {0: 8, 1: 4, 2: 8, 3: 4}
    PC_SPLIT = {0: (8, 12, 4), 1: (8, 4), 2: (8, 12, 4), 3: (8, 4)}

    if phases is None:
        phases = [("mod",)]
        for l in range(DEPTH):
            phases.append(("mix", l))
            phases.append(("moe", l))

    with ExitStack() as es:
        ARENA_BYTES = 206 * 1024
        arena_t = es.enter_context(nc.sbuf_tensor("arena", [128, ARENA_BYTES // 2], BF16))
        AR = Arena(arena_t[:], ARENA_BYTES)
        pbanks = [es.enter_context(nc.psum_tensor("pb%d" % i, [128, 512], F32)) for i in range(8)]
        ctx = SemCtx(nc, es)

        def PSf(i, shape=None):
            v = pbanks[i][:]
            if shape is not None and len(shape) == 3:
                n = shape[1] * shape[2]
                v = v[0:shape[0], 0:n].rearrange("p (a b) -> p a b", b=shape[2])
            elif shape is not None:
                v = v[0:shape[0], 0:shape[1]]
            return v

        def PSb(i, shape):
            v = pbanks[i][:].bitcast(BF16)
            n = 1
            for s_ in shape[1:]:
                n *= s_
            v = v[0:shape[0], 0:n]
            if len(shape) == 3:
                v = v.rearrange("p (a b) -> p a b", b=shape[2])
            return v

        pers = Bump(AR, 0, 80 * 1024)
        xs = pers.alloc([128, NT, D], F32)
        mvA = pers.alloc([128, D], F32)
        mvB = pers.alloc([128, D], F32)
        mvC = pers.alloc([128, D], F32)
        ident = pers.alloc([128, 128], BF16)
        identf = pers.alloc([128, 128], F32)
        bones = pers.alloc([128, 128], BF16)
        epst = pers.alloc([128, 1], F32)
        sc12 = pers.alloc([128, KC, 12, 12], BF16)
        PH0 = 80 * 1024
        PHN = ARENA_BYTES

        xs_v = x_d.rearrange("(t p) d -> p t d", p=128)
        out_v = out_d.rearrange("(t p) d -> p t d", p=128)

        def phase_mod():
            p = Prog(nc, ctx)
            b = Bump(AR, PH0, PHN)
            scol = b.alloc([128, 8], F32)
            scolb = b.alloc([128, 8], BF16)
            wsl = [b.alloc([128, KC, 512], BF16) for _ in range(4)]
            for q4 in range(4):
                p.dma("sp", lambda e, q4=q4: e.dma_start(out=xs[:, q4 * 4:(q4 + 1) * 4, :], in_=xs_v[:, q4 * 4:(q4 + 1) * 4, :]),
                      write=("xs", q4))
            p.add("pool", lambda e: e.memset(identf, 0.0), writes=["identf"])
            p.add("pool", lambda e: e.affine_select(out=identf, in_=identf, pattern=[[-1, 128]], compare_op=ALU.not_equal,
                                                   fill=1.0, base=0, channel_multiplier=1),
                  reads=["identf"], writes=["identf"])
            p.add("dve", lambda e: e.tensor_copy(out=ident, in_=identf), reads=["identf"], writes=["ident"])
            p.add("pool", lambda e: e.memset(epst, EPS), writes=["epst"])
            p.add("pool", lambda e: e.memset(bones, 0.0), writes=["bones"])
            p.add("pool", lambda e: e.memset(bones[0:64, 0:64], 1.0), reads=["bones"], writes=["bones"])
            p.add("pool", lambda e: e.memset(bones[64:128, 64:128], 1.0), reads=["bones"], writes=["bones"])
            p.dma("sp", lambda e: e.dma_start(out=scol, in_=cT_d), write="scol")
            p.add("act", lambda e: e.activation(out=scolb, in_=scol, func=AF.Silu), reads=["scol"], writes=["scolb"])
            p.add("pool", lambda e: e.memset(sc12, 0.0), writes=["sc12"])
            for bb in range(12):
                p.add("dve", lambda e, bb=bb: e.tensor_copy(out=sc12[:, :, bb, bb], in_=scolb), reads=["scolb", "sc12"], writes=[("sc12c", bb)])
            sck = [("sc12c", bb) for bb in range(12)]
            brow12 = b.alloc([12, 512], F32)
            p.dma("sp", lambda e: e.dma_start(out=brow12, in_=modb_d[0].rearrange("(a b) -> a b", b=512)), write="brow12")
            wkeys = RR(range(4))
            for blk in range(12):
                ws = wkeys.next()
                p.dma("pool", lambda e, blk=blk, ws=ws: e.dma_start(
                    out=wsl[ws], in_=modw_d[0, :, blk * 512:(blk + 1) * 512].rearrange("(k p) n -> p k n", p=128)),
                    write=("wsl", ws))
                def mm(e, blk=blk, ws=ws):
                    ins = None
                    for k in range(KC):
                        ins = e.matmul(PSf(7, [12, 512]), lhsT=sc12[:, k, blk, :], rhs=wsl[ws][:, k, :],
                                       start=(blk == 0 and k == 0), stop=(blk == 11 and k == KC - 1))
                    return ins
                p.add("pe", mm, reads=sck + [("wsl", ws)], writes=[("ps", 7)])
            p.add("dve", lambda e: e.tensor_tensor(out=brow12, in0=PSf(7, [12, 512]), in1=brow12, op=ALU.add),
                  reads=[("ps", 7), "brow12"], writes=["brow12"])
            p.dma("sp", lambda e: e.dma_start(out=modv_d[0].rearrange("(a b) -> a b", b=512), in_=brow12), reads=["brow12"], write=("modv", 0))
            p.finish()

        pc_state = {}

        def precast(p, l, part):
            e0 = NE - PC_EXPERTS[l]
            mats = [(e_, i_) for e_ in range(e0, NE) for i_ in range(3)]
            start = sum(PC_SPLIT[l][:part])
            for n_, (e_, i_) in enumerate(mats[start:start + PC_SPLIT[l][part]]):
                wsrc = (wg_d, wu_d, wd_d)[i_]
                p.dma("pool", lambda e, e_=e_, i_=i_, wsrc=wsrc: e.dma_start(out=wbf_d[e_ * 3 + i_], in_=wsrc[l, e_]),
                      write=("pc", n_ % 4))

        def load_modvec(p, l, which):
            base = 0 if which == 0 else 3 * D
            p.dma("sp", lambda e: e.dma_start(out=mvB, in_=modv_d[l:l + 1, base:base + D].partition_broadcast(128)[:, 0, :]), write="mvB")
            p.dma("sp", lambda e: e.dma_start(out=mvA, in_=modv_d[l:l + 1, base + D:base + 2 * D].partition_broadcast(128)[:, 0, :]), write="mvA")
            p.dma("sp", lambda e: e.dma_start(out=mvC, in_=ng_d[2 * l + which:2 * l + which + 1, :].partition_broadcast(128)[:, 0, :]), write="mvC")
            p.add("dve", lambda e: e.scalar_tensor_tensor(out=mvA, in0=mvA, scalar=1.0, in1=mvC, op0=ALU.add, op1=ALU.mult),
                  reads=["mvA", "mvC"], writes=["mvA"])

        def load_gate(p, l, which):
            base = 2 * D if which == 0 else 5 * D
            p.dma("sp", lambda e: e.dma_start(out=mvC, in_=modv_d[l:l + 1, base:base + D].partition_broadcast(128)[:, 0, :]), write="mvC")

        def norm_stats(p, b):
            ss = b.alloc([128, NT], F32)
            rstd = b.alloc([128, NT], F32)
            junk = b.alloc([128, D], BF16)
            p.add("dve", lambda e: e.memset(ss, 0.0), writes=["ss"])
            for t in range(NT):
                p.add("act", lambda e, t=t: e.activation(out=junk, in_=xs[:, t, :], func=AF.Square, accum_out=ss[:, t:t + 1]),
                      reads=["ss"], writes=[("ssc", t)])
            p.add("act", lambda e: e.activation(out=rstd, in_=ss, func=AF.Sqrt, bias=epst, scale=1.0 / D),
                  reads=[("ssc", t) for t in range(NT)] + ["ss"], writes=["rstd"])
            p.add("dve", lambda e: e.reciprocal(out=rstd, in_=rstd), reads=["rstd"], writes=["rstd"])
            return rstd

        def norm_chunk(p, t, rstd, tmp, tmpk, dst, dstk):
            o = p.add("dve", lambda e: e.scalar_tensor_tensor(out=tmp, in0=xs[:, t, :], scalar=rstd[:, t:t + 1], in1=mvA,
                                                              op0=ALU.mult, op1=ALU.mult),
                      reads=["rstd", "mvA"], writes=[tmpk])
            p.add("dve", lambda e: e.tensor_tensor(out=dst, in0=tmp, in1=mvB, op=ALU.add),
                  reads=[tmpk, "mvB"], writes=[dstk])
            return o

        def out_proj(p, l, wo_src, lhs_of, b, after_load=None):
            wo = b.alloc([128, KC, D], BF16)
            p.dma("pool", lambda e: e.dma_start(out=wo, in_=wo_src.rearrange("(k p) n -> p k n", p=128)), write="wo")
            if after_load is not None:
                after_load()
            load_gate(p, l, 0)
            for k in range(KC):
                eng = "dve"
                p.add(eng, lambda e, k=k: e.tensor_tensor(out=wo[:, k, :], in0=wo[:, k, :], in1=mvC, op=ALU.mult),
                      reads=["wo", "mvC"], writes=[("wos", k)])
            pk = RR(range(8))
            for t in range(NT):
                for hf in range(2):
                    pb = pk.next()
                    _mm(p, PSf(pb), [(lhs_of(k, t), wo[:, k, hf * 512:(hf + 1) * 512]) for k in range(KC)],
                        reads=["catT"] + [("wos", k) for k in range(KC)], writes=[("ps", pb)])
                    p.add("dve", lambda e, t=t, hf=hf, pb=pb: e.tensor_tensor(
                        out=xs[:, t, hf * 512:(hf + 1) * 512], in0=xs[:, t, hf * 512:(hf + 1) * 512], in1=PSf(pb), op=ALU.add),
                        reads=[("ps", pb)], writes=[("xs", t, hf)])

        def phase_mix_even(l):
            j = l // 2
            K = 1024
            convT = AR.view(PH0, [128, 4, S], BF16)
            hT = AR.view(PH0 + 16 * K, [128, KC, S], BF16)
            qT = AR.view(PH0 + 48 * K, [128, 4, S], BF16)
            kT = AR.view(PH0 + 64 * K, [128, 4, S], BF16)
            vaug = AR.view(PH0 + 80 * K, [128, NT, 8, 65], BF16)
            p = Prog(nc, ctx)
            b = Bump(AR, PH0 + 48 * K, PHN)
            load_modvec(p, l, 0)
            rstd = norm_stats(p, b)
            tmps = [b.alloc([128, D], F32) for _ in range(2)]
            hcs = [b.alloc([128, D], BF16) for _ in range(2)]
            wsl = [b.alloc([128, KC, 512], BF16) for _ in range(3)]
            cwt = b.alloc([128, 12], F32)
            qkg = b.alloc([128, 2], F32)
            us = [b.alloc([128, S + 2], BF16) for _ in range(2)]
            bgs = [b.alloc([128, S], BF16) for _ in range(2)]
            cgs = [b.alloc([128, 512], F32) for _ in range(2)]
            zs = [b.alloc([128, 512], F32) for _ in range(2)]
            z1s = [b.alloc([128, 512], F32) for _ in range(2)]
            p.dma("sp", lambda e: e.dma_start(out=cwt, in_=cw_d[j]), write="cwt")
            for i, sec in enumerate((3, 4, 5)):
                p.dma("pool", lambda e, i=i, sec=sec: e.dma_start(
                    out=wsl[i], in_=win_d[j, :, sec * 512:(sec + 1) * 512].rearrange("(k p) n -> p k n", p=128)), write=("wsl", i))
            for i in range(2):
                p.add("pool", lambda e, i=i: e.memset(us[i][:, 0:1], 0.0), writes=[("upad", i)])
                p.add("pool", lambda e, i=i: e.memset(us[i][:, S + 1:S + 2], 0.0), writes=[("upad2", i)])
            precast(p, l, 0)
            for t in range(NT):
                s2 = t % 2
                norm_chunk(p, t, rstd, tmps[s2], ("tmp", s2), hcs[s2], ("hc", s2))
                for kh in range(2):
                    pb = (t * 2 + kh) % 2
                    def tr(e, s2=s2, kh=kh, pb=pb):
                        ins = None
                        for jj in range(4):
                            k = kh * 4 + jj
                            ins = e.transpose(out=PSb(pb, [128, 4, 128])[:, jj, :], in_=hcs[s2][:, k * 128:(k + 1) * 128], identity=ident)
                        return ins
                    p.add("pe", tr, reads=[("hc", s2), "ident"], writes=[("ps", pb)])
                    eng = "act" if kh == 0 else "dve"
                    if eng == "act":
                        p.add("act", lambda e, t=t, kh=kh, pb=pb: e.copy(out=hT[:, kh * 4:(kh + 1) * 4, t * 128:(t + 1) * 128], in_=PSb(pb, [128, 4, 128])),
                              reads=[("ps", pb)], writes=[("hT", t)])
                    else:
                        p.add("dve", lambda e, t=t, kh=kh, pb=pb: e.tensor_copy(out=hT[:, kh * 4:(kh + 1) * 4, t * 128:(t + 1) * 128], in_=PSb(pb, [128, 4, 128])),
                              reads=[("ps", pb)], writes=[("hT", t)])
            hT_all = [("hT", t) for t in range(NT)]
            pk = RR(range(2, 8))
            for cc in range(4):
                ub = cc % 2
                u, bg = us[ub], bgs[ub]
                for n in range(4):
                    pcg, pxv, pbg = pk.next(), pk.next(), pk.next()
                    hkeys = [("hT", t) for t in range(n * 4, n * 4 + 4)]
                    cs_ = n % 2
                    _mm(p, PSf(pcg), [(wsl[1][:, k, cc * 128:(cc + 1) * 128], hT[:, k, n * 512:(n + 1) * 512]) for k in range(KC)],
                        reads=hkeys + [("wsl", 1)], writes=[("ps", pcg)])
                    p.add("act", lambda e, pcg=pcg, cs_=cs_: e.copy(out=cgs[cs_], in_=PSf(pcg)), reads=[("ps", pcg)], writes=[("cgs", cs_)])
                    _mm(p, PSf(pxv), [(wsl[2][:, k, cc * 128:(cc + 1) * 128], hT[:, k, n * 512:(n + 1) * 512]) for k in range(KC)],
                        reads=hkeys + [("wsl", 2)], writes=[("ps", pxv)])
                    p.add("dve", lambda e, pxv=pxv, cs_=cs_, u=u, n=n: e.tensor_tensor(
                        out=u[:, 1 + n * 512:1 + (n + 1) * 512], in0=PSf(pxv), in1=cgs[cs_], op=ALU.mult),
                        reads=[("ps", pxv), ("cgs", cs_)], writes=[("u", ub, n)])
                    _mm(p, PSf(pbg), [(wsl[0][:, k, cc * 128:(cc + 1) * 128], hT[:, k, n * 512:(n + 1) * 512]) for k in range(KC)],
                        reads=hkeys + [("wsl", 0)], writes=[("ps", pbg)])
                    p.add("act", lambda e, pbg=pbg, bg=bg, n=n: e.copy(out=bg[:, n * 512:(n + 1) * 512], in_=PSf(pbg)),
                          reads=[("ps", pbg)], writes=[("bg", ub, n)])
                ukeys = [("u", ub, n) for n in range(4)] + [("upad", ub), ("upad2", ub)]
                for n in range(4):
                    zi = n % 2
                    z = zs[zi]
                    z1 = z1s[zi]
                    p.add("dve", lambda e, z=z, u=u, n=n, cc=cc: e.tensor_scalar(
                        out=z, in0=u[:, n * 512:n * 512 + 512], scalar1=cwt[:, cc * 3 + 0:cc * 3 + 1], scalar2=None, op0=ALU.mult),
                        reads=ukeys + ["cwt"], writes=[("z", zi)])
                    p.add("dve", lambda e, z=z, u=u, n=n, cc=cc: e.scalar_tensor_tensor(
                        out=z, in0=u[:, 1 + n * 512:1 + n * 512 + 512], scalar=cwt[:, cc * 3 + 1:cc * 3 + 2], in1=z, op0=ALU.mult, op1=ALU.add),
                        reads=ukeys + [("z", zi), "cwt"], writes=[("z", zi)])
                    p.add("dve", lambda e, z=z, u=u, n=n, cc=cc: e.scalar_tensor_tensor(
                        out=z, in0=u[:, 2 + n * 512:2 + n * 512 + 512], scalar=cwt[:, cc * 3 + 2:cc * 3 + 3], in1=z, op0=ALU.mult, op1=ALU.add),
                        reads=ukeys + [("z", zi), "cwt"], writes=[("z", zi)])
                    p.add("dve", lambda e, z=z, bg=bg, n=n, cc=cc: e.tensor_tensor(
                        out=convT[:, cc, n * 512:(n + 1) * 512], in0=z, in1=bg[:, n * 512:(n + 1) * 512], op=ALU.mult),
                        reads=[("z", zi), ("bg", ub, n)], writes=[("convT", cc, n)])
            p.finish()

            p = Prog(nc, ctx)
            b = Bump(AR, PH0 + 97 * K, PHN)
            wsl2 = [b.alloc([128, KC, 512], BF16) for _ in range(2)]
            wsl = [wsl2[0], wsl2[1], wsl2[0]]
            qkg = b.alloc([128, 2], F32)
            qkg2 = b.alloc([128, 2], F32)
            sqs = [b.alloc([128, 512], BF16) for _ in range(2)]
            rrs = [b.alloc([128, 512], F32) for _ in range(2)]
            for i, sec in enumerate((0, 1, 2)):
                if i == 2:
                    continue
                p.dma("pool", lambda e, i=i, sec=sec: e.dma_start(
                    out=wsl[i], in_=win_d[j, :, sec * 512:(sec + 1) * 512].rearrange("(k p) n -> p k n", p=128)), write=("wsl", i % 2))
            p.dma("sp", lambda e: e.dma_start(out=qkg, in_=qkg_d[j]), write="qkg")
            p.add("dve", lambda e: e.tensor_scalar(out=qkg2[:, 0:1], in0=qkg[:, 0:1], scalar1=0.125, scalar2=None, op0=ALU.mult),
                  reads=["qkg"], writes=["qkg2a"])
            p.add("dve", lambda e: e.tensor_scalar(out=qkg2[:, 1:2], in0=qkg[:, 1:2], scalar1=1.0, scalar2=None, op0=ALU.mult),
                  reads=["qkg"], writes=["qkg2b"])
            p.add("pool", lambda e: e.memset(vaug[:, :, :, 64:65], 1.0), writes=["vones"])
            pk = RR(range(0, 6))
            pk2 = RR(range(6, 8))
            tiles = [(si, dst, pc, n) for si, dst in ((0, qT), (1, kT)) for pc in range(4) for n in range(4)]
            pqs = [pk.next() for _ in tiles]

            def qk_front(i):
                si, dst, pc, n = tiles[i]
                pq, s2 = pqs[i], i % 2
                _mm(p, PSf(pq), [(wsl[si][:, k, pc * 128:(pc + 1) * 128], hT[:, k, n * 512:(n + 1) * 512]) for k in range(KC)],
                    reads=[("wsl", si % 2)], writes=[("ps", pq)])
                p.add("act", lambda e: e.activation(out=sqs[s2], in_=PSf(pq), func=AF.Square),
                      reads=[("ps", pq)], writes=[("sq", s2)])

            def qk_back(i):
                si, dst, pc, n = tiles[i]
                pq, s2 = pqs[i], i % 2
                pss = pk2.next()
                _mm(p, PSf(pss), [(bones, sqs[s2])], reads=[("sq", s2)], writes=[("ps", pss)])
                p.add("act", lambda e: e.activation(out=rrs[s2], in_=PSf(pss), func=AF.Ln, bias=epst, scale=1.0 / 64),
                      reads=[("ps", pss)], writes=[("rr", s2)])
                p.add("act", lambda e: e.activation(out=rrs[s2], in_=rrs[s2], func=AF.Exp, scale=-0.5),
                      reads=[("rr", s2)], writes=[("rr", s2)])
                p.add("dve", lambda e: e.scalar_tensor_tensor(
                    out=dst[:, pc, n * 512:(n + 1) * 512], in0=PSf(pq), scalar=qkg2[:, si:si + 1], in1=rrs[s2],
                    op0=ALU.mult, op1=ALU.mult),
                    reads=[("ps", pq), ("rr", s2), "qkg2a", "qkg2b"], writes=[("qk", si, pc, n)])

            for i in range(len(tiles)):
                qk_front(i)
                if i >= 1:
                    qk_back(i - 1)
            qk_back(len(tiles) - 1)
            p.dma("pool", lambda e: e.dma_start(
                out=wsl[2], in_=win_d[j, :, 2 * 512:3 * 512].rearrange("(k p) n -> p k n", p=128)), write=("wsl", 0))
            for t in range(NT):
                pv = pk.next()
                _mm(p, PSf(pv), [(hT[:, k, t * 128:(t + 1) * 128], wsl[2][:, k, :]) for k in range(KC)],
                    reads=[("wsl", 0)], writes=[("ps", pv)])
                p.add("act", lambda e, t=t, pv=pv: e.copy(out=vaug[:, t, :, 0:64], in_=PSf(pv, [128, 8, 64])),
                      reads=[("ps", pv)], writes=[("v", t)])
            p.finish()

            p = Prog(nc, ctx)
            b = Bump(AR, PH0 + 16 * K, PH0 + 48 * K)
            attn_tok = b.alloc([128, NT, 512], BF16)
            tmpb = [b.alloc([128, 5, 128], F32) for _ in range(3)]
            ebs = [b.alloc([128, 5, 128], BF16) for _ in range(3)]
            recs = [b.alloc([128, 1], F32) for _ in range(4)]
            b2 = Bump(AR, PH0 + 97 * K, PHN)
            bias = [b2.alloc([128, 5, 5, 128], F32) for _ in range(2)]
            precast(p, l, 1)
            def att_front(h, t, it):
                pc, hf = h // 2, h % 2
                bs = h % 2
                c0, nk = _att_chunks(t)
                cls = _att_cls(t)
                s2 = it % 3
                psA, psB = 2 * s2, 2 * s2 + 1
                def qk(e):
                    ins = None
                    for jj in range(nk):
                        outp = PSf(psA, [128, 4, 128])[:, jj, :] if jj < 4 else PSf(psB, [128, 4, 128])[:, 0, :]
                        ins = e.matmul(outp, lhsT=kT[hf * 64:(hf + 1) * 64, pc, (c0 + jj) * 128:(c0 + jj + 1) * 128],
                                       rhs=qT[hf * 64:(hf + 1) * 64, pc, t * 128:(t + 1) * 128], start=True, stop=True)
                    return ins
                p.add("pe", qk, reads=[], writes=[("ps", psA), ("ps", psB)])
                n4 = min(nk, 4)
                p.add("dve", lambda e: e.tensor_tensor(
                    out=tmpb[s2][:, 0:n4, :], in0=PSf(psA, [128, 4, 128])[:, 0:n4, :], in1=bias[bs][:, cls, 0:n4, :], op=ALU.add),
                    reads=[("ps", psA), ("bias", bs)], writes=[("tmpb", s2)])
                if nk == 5:
                    p.add("dve", lambda e: e.tensor_tensor(
                        out=tmpb[s2][:, 4, :], in0=PSf(psB, [128, 4, 128])[:, 0, :], in1=bias[bs][:, cls, 4, :], op=ALU.add),
                        reads=[("ps", psB), ("bias", bs), ("tmpb", s2)], writes=[("tmpb", s2)])
                p.add("act", lambda e: e.activation(out=ebs[s2][:, 0:nk, :], in_=tmpb[s2][:, 0:nk, :], func=AF.Exp),
                      reads=[("tmpb", s2)], writes=[("eb", s2)])

            def att_back(h, t, it):
                c0, nk = _att_chunks(t)
                s2 = it % 3
                pso = 6 + (it % 2)
                rc = it % 4
                def pv(e):
                    ins = None
                    for jj in range(nk):
                        ins = e.matmul(PSf(pso, [128, 65]), lhsT=ebs[s2][:, jj, :], rhs=vaug[:, c0 + jj, h, :],
                                       start=(jj == 0), stop=(jj == nk - 1))
                    return ins
                p.add("pe", pv, reads=[("eb", s2)], writes=[("ps", pso)])
                p.add("dve", lambda e: e.reciprocal(out=recs[rc], in_=PSf(pso, [128, 65])[:, 64:65]),
                      reads=[("ps", pso)], writes=[("rec", rc)])
                p.add("act", lambda e: e.activation(
                    out=attn_tok[:, t, h * 64:(h + 1) * 64], in_=PSf(pso, [128, 65])[:, 0:64], func=AF.Copy, scale=recs[rc]),
                    reads=[("ps", pso), ("rec", rc)], writes=[("attn", t)])

            seq = [(h, t) for h in range(8) for t in range(NT)]
            def load_bias(h):
                bs = h % 2
                p.dma("sp", lambda e, h=h, bs=bs: e.dma_start(out=bias[bs], in_=btab_d[j, h].rearrange("p (a b c) -> p a b c", a=5, b=5)),
                      write=("bias", bs))
            load_bias(0)
            for i, (h, t) in enumerate(seq):
                if t == 0 and h + 1 < 8:
                    load_bias(h + 1)
                att_front(h, t, i)
                if i >= 2:
                    att_back(seq[i - 2][0], seq[i - 2][1], i - 2)
            for i in (len(seq) - 2, len(seq) - 1):
                att_back(seq[i][0], seq[i][1], i)
            p.finish()

            p = Prog(nc, ctx)
            attnT = AR.view(PH0 + 48 * K, [128, 4, S], BF16)
            b = Bump(AR, PH0 + 64 * K, PHN)
            for t in range(NT):
                pb = t % 2
                def tr(e, t=t, pb=pb):
                    ins = None
                    for jj in range(4):
                        ins = e.transpose(out=PSb(pb, [128, 4, 128])[:, jj, :], in_=attn_tok[:, t, jj * 128:(jj + 1) * 128], identity=ident)
                    return ins
                p.add("pe", tr, reads=[], writes=[("ps", pb)])
                if t % 2 == 0:
                    p.add("act", lambda e, t=t, pb=pb: e.copy(out=attnT[:, :, t * 128:(t + 1) * 128], in_=PSb(pb, [128, 4, 128])),
                          reads=[("ps", pb)], writes=["catT"])
                else:
                    p.add("dve", lambda e, t=t, pb=pb: e.tensor_copy(out=attnT[:, :, t * 128:(t + 1) * 128], in_=PSb(pb, [128, 4, 128])),
                          reads=[("ps", pb)], writes=["catT"])

            def lhs_of(k, t):
                return attnT[:, k, t * 128:(t + 1) * 128] if k < 4 else convT[:, k - 4, t * 128:(t + 1) * 128]
            out_proj(p, l, wout_d[j], lhs_of, b, after_load=lambda: precast(p, l, 2))
            p.finish()

        def phase_mix_odd(l):
            j = l // 2
            K = 1024
            htok = AR.view(PH0, [128, NT, D], BF16)
            fT = AR.view(PH0, [128, KC, S], BF16)
            GrT = AR.view(PH0 + 32 * K, [128, KC, S], BF16)
            GsT = AR.view(PH0 + 64 * K, [128, KC, S], BF16)
            p = Prog(nc, ctx)
            b = Bump(AR, PH0 + 32 * K, PH0 + 64 * K)
            load_modvec(p, l, 0)
            rstd = norm_stats(p, b)
            tmps = [b.alloc([128, D], F32) for _ in range(2)]
            for t in range(NT):
                s2 = t % 2
                norm_chunk(p, t, rstd, tmps[s2], ("tmp", s2), htok[:, t, :], ("htok", t))
            b = Bump(AR, PH0 + 96 * K, PHN)
            NSL = 3
            dsl = [(b.alloc([128, 4, 512], BF16), b.alloc([128, 4, 512], BF16)) for _ in range(NSL)]
            nyq = b.alloc([128, NT], BF16)
            precast(p, l, 0)
            p.dma("sp", lambda e: e.dma_start(out=nyq, in_=nyq_d), write="nyq")
            si = 0
            for half in range(2):
                for n in range(2):
                    for kg in range(4):
                        sl = si % NSL
                        si += 1
                        p.dma("sp", lambda e, sl=sl, kg=kg, n=n: e.dma_start(
                            out=dsl[sl][0], in_=dcos_d[kg * 512:(kg + 1) * 512, n * 512:(n + 1) * 512].rearrange("(k p) s -> p k s", p=128)),
                            write=("dc", sl))
                        p.dma("sp", lambda e, sl=sl, kg=kg, n=n: e.dma_start(
                            out=dsl[sl][1], in_=dsin_d[kg * 512:(kg + 1) * 512, n * 512:(n + 1) * 512].rearrange("(k p) s -> p k s", p=128)),
                            write=("ds", sl))
                        def mm(e, sl=sl, kg=kg, half=half):
                            ins = None
                            for kk in range(4):
                                k = kg * 4 + kk
                                for jj in range(4):
                                    dch = half * 4 + jj
                                    lhs = htok[:, k, dch * 128:(dch + 1) * 128]
                                    e.matmul(PSf(jj), lhsT=lhs, rhs=dsl[sl][0][:, kk, :], start=(k == 0), stop=(k == NT - 1))
                                    ins = e.matmul(PSf(4 + jj), lhsT=lhs, rhs=dsl[sl][1][:, kk, :], start=(k == 0), stop=(k == NT - 1))
                            return ins
                        p.add("pe", mm, reads=[("dc", sl), ("ds", sl)] + [("htok", kg * 4 + kk) for kk in range(4)]
                              + ([("evac", i) for i in range(8)] if kg == 0 else []),
                              writes=[("acc",)] if kg < 3 else [("ps", i) for i in range(8)])
                    for jj in range(4):
                        dch = half * 4 + jj
                        p.add("act", lambda e, jj=jj, dch=dch, n=n: e.copy(out=GrT[:, dch, n * 512:(n + 1) * 512], in_=PSf(jj)),
                              reads=[("ps", jj)], writes=[("evac", jj), "GT"])
                        p.add("dve", lambda e, jj=jj, dch=dch, n=n: e.tensor_copy(out=GsT[:, dch, n * 512:(n + 1) * 512], in_=PSf(4 + jj)),
                              reads=[("ps", 4 + jj)], writes=[("evac", 4 + jj), "GT2"])
            for dch in range(KC):
                _mm(p, PSf(0, [128, KC])[:, dch:dch + 1], [(htok[:, k, dch * 128:(dch + 1) * 128], nyq[:, k:k + 1]) for k in range(NT)],
                    reads=["nyq"] + [("htok", t) for t in range(NT)], writes=[("ps", 0)])
            p.add("act", lambda e: e.copy(out=GrT[:, :, 1024:1025], in_=PSf(0, [128, KC]).unsqueeze(2)), reads=[("ps", 0)], writes=["GTn"])
            p.add("dve", lambda e: e.memset(GsT[:, :, 1024:1025], 0.0), writes=["GT2n"])
            p.add("act", lambda e: e.copy(out=GrT[:, :, 1025:2048], in_=GrT[:, :, 1023:0:-1]), reads=["GT"], writes=["GTm"])
            p.add("dve", lambda e: e.tensor_scalar(out=GsT[:, :, 1025:2048], in0=GsT[:, :, 1023:0:-1], scalar1=-1.0, scalar2=None, op0=ALU.mult),
                  reads=["GT2"], writes=["GT2m"])
            p.finish()
            p = Prog(nc, ctx)
            b = Bump(AR, PH0 + 96 * K, PHN)
            cct = b.alloc([128, 2, 256], BF16)
            cst = b.alloc([128, 2, 256], BF16)
            p.dma("sp", lambda e: e.dma_start(out=cct, in_=cc_d.rearrange("(k p) n -> p k n", p=128)), write="cct")
            p.dma("sp", lambda e: e.dma_start(out=cst, in_=cs_d.rearrange("(k p) n -> p k n", p=128)), write="cst")
            pk = RR(range(8))
            idx = 0
            for g in range(4):
                for j2 in range(2):
                    for n in range(4):
                        pb = pk.next()
                        pairs = []
                        for i2 in range(2):
                            pairs.append((cct[:, i2, j2 * 128:(j2 + 1) * 128], GrT[:, 2 * g + i2, n * 512:(n + 1) * 512]))
                            pairs.append((cst[:, i2, j2 * 128:(j2 + 1) * 128], GsT[:, 2 * g + i2, n * 512:(n + 1) * 512]))
                        _mm(p, PSf(pb), pairs, reads=["cct", "cst"], writes=[("ps", pb)])
                        if idx % 2 == 0:
                            p.add("act", lambda e, pb=pb, g=g, j2=j2, n=n: e.copy(out=fT[:, 2 * g + j2, n * 512:(n + 1) * 512], in_=PSf(pb)),
                                  reads=[("ps", pb)], writes=["catT"])
                        else:
                            p.add("dve", lambda e, pb=pb, g=g, j2=j2, n=n: e.tensor_copy(out=fT[:, 2 * g + j2, n * 512:(n + 1) * 512], in_=PSf(pb)),
                                  reads=[("ps", pb)], writes=["catT"])
                        idx += 1
            b3 = Bump(AR, PH0 + 32 * K, PH0 + 96 * K)
            b3 = Bump(AR, PH0 + 98 * K, PHN)
            out_proj(p, l, fw_d[j], lambda k, t: fT[:, k, t * 128:(t + 1) * 128], b3, after_load=lambda: precast(p, l, 1))
            p.finish()

        def phase_moe(l, last):
            K = 1024
            p = Prog(nc, ctx)
            NSLOT = 10
            wb = Bump(AR, PH0, PH0 + 80 * K)
            wsl = [wb.alloc([128, KC, 512], BF16) for _ in range(NSLOT)]
            eb = Bump(AR, PH0 + 80 * K, PHN)
            xes = [eb.alloc([128, 2, D], BF16) for _ in range(3)]
            xeTs = [eb.alloc([128, KC, CAP], BF16) for _ in range(2)]
            hids = [eb.alloc([128, KC, CAP], BF16) for _ in range(1)] * 2
            ysbs = [eb.alloc([128, 2, D], F32) for _ in range(2)]
            e_end = eb.cur
            sas = [eb.alloc([128, CAP], F32) for _ in range(2)]
            idx_tok = eb.alloc([128, 2, NE], I32)
            gate_tok = eb.alloc([128, 2, NE], F32)
            rwt = eb.alloc([128, KC, NE], BF16)
            r1 = Bump(AR, PH0 + 80 * K, PH0 + 104 * K)
            r2 = Bump(AR, PH0 + 112 * K, PH0 + 120 * K)
            affT = r1.alloc([NE, S], F32)
            tmps = [r1.alloc([128, D], F32) for _ in range(1)] * 2
            hcs = [r1.alloc([128, D], BF16) for _ in range(4)]
            hTc = [r1.alloc([128, KC, 128], BF16) for _ in range(2)]
            vals = r2.alloc([NE, CAP], F32)
            idxu = r2.alloc([NE, CAP], U32)
            idxf = r2.alloc([NE, CAP], F32)
            ss = r2.alloc([128, NT], F32)
            rstd = r2.alloc([128, NT], F32)
            junk = r2.alloc([128, D], BF16)
            lgs = r2.alloc([128, NT, NE], F32)
            mask_hi = r2.alloc([NE, 1], I32)
            mask_lo = r2.alloc([NE, 1], I32)
            mxall = r2.alloc([128, NT], F32)
            small = r2.alloc([128, NT], F32)

            has_mod = (l + 1 < DEPTH) and not NO_MOD_RING
            srcs = []
            if has_mod:
                for blk in range(12):
                    srcs.append(modw_d[l + 1, :, blk * 512:(blk + 1) * 512])
            EB = len(srcs)
            for e_ in range(NE):
                for i_, wsrc in enumerate((wg_d, wu_d, wd_d)):
                    for hh in range(2):
                        if e_ >= NE - PC_EXPERTS[l]:
                            srcs.append(wbf_d[e_ * 3 + i_][hh * 512:(hh + 1) * 512, :])
                        else:
                            srcs.append(wsrc[l, e_][hh * 512:(hh + 1) * 512, :])
            NXS = 8
            wsl_all = list(wsl) + [AR.view(i * 8 * K, [128, KC, 512], BF16) for i in range(NXS)]
            wslr = [w_.rearrange("p k n -> p (k n)").rearrange("p (k n) -> p k n", n=D) for w_ in wsl_all]
            NTOT = NSLOT + NXS
            slot_of, prev_occ = [], []
            free_arena = [(s_, None) for s_ in range(NSLOT)]
            xs_free = list(range(NSLOT, NTOT))
            k_ = 0
            for m in range(len(srcs)):
                while not free_arena and (m < EB or not xs_free):
                    if slot_of[k_] < NSLOT:
                        free_arena.append((slot_of[k_], k_))
                    k_ += 1
                if free_arena:
                    s_, pv_ = free_arena.pop(0)
                else:
                    s_, pv_ = xs_free.pop(0), None
                slot_of.append(s_)
                prev_occ.append(pv_)
            xs_open = [False]
            xs_gate = []
            xs_first = set()
            w_issued = [0]
            done_upto = [-1]

            def pump():
                while w_issued[0] < len(srcs):
                    m = w_issued[0]
                    s_ = slot_of[m]
                    if prev_occ[m] is not None and prev_occ[m] > done_upto[0]:
                        break
                    after = ()
                    if s_ >= NSLOT:
                        if not xs_open[0]:
                            break
                        if s_ not in xs_first:
                            xs_first.add(s_)
                            after = tuple(xs_gate)
                    dst = wsl_all[s_] if m < EB else wslr[s_]
                    p.dma("pool", lambda e, m=m, dst=dst: e.dma_start(out=dst, in_=srcs[m].rearrange("(k p) n -> p k n", p=128)),
                          write=("w", s_), after=after)
                    w_issued[0] += 1

            def consumed(m):
                done_upto[0] = max(done_upto[0], m)
                pump()

            p.dma("pool", lambda e: e.dma_start(out=rwt, in_=rw_d[l].rearrange("(k p) n -> p k n", p=128)), write="rwt")
            pump()
            if has_mod:
                brow12 = eb.alloc([12, 512], F32)
                p.dma("sp", lambda e: e.dma_start(out=brow12, in_=modb_d[l + 1].rearrange("(a b) -> a b", b=512)), write="brow12")

            def mod_block(blk):
                sl = slot_of[blk]
                def mm(e):
                    ins = None
                    for k in range(KC):
                        ins = e.matmul(PSf(7, [12, 512]), lhsT=sc12[:, k, blk, :], rhs=wsl_all[sl][:, k, :],
                                       start=(blk == 0 and k == 0), stop=(blk == 11 and k == KC - 1))
                    return ins
                p.add("pe", mm, reads=[("w", sl)], writes=[("ps", 7)])
                consumed(blk)
                if blk == 11:
                    p.add("dve", lambda e: e.tensor_tensor(out=brow12, in0=PSf(7, [12, 512]), in1=brow12, op=ALU.add),
                          reads=[("ps", 7), "brow12"], writes=["brow12"])
                    p.dma("sp", lambda e: e.dma_start(out=modv_d[l + 1].rearrange("(a b) -> a b", b=512), in_=brow12),
                          reads=["brow12"], write=("modv", l + 1))
            load_modvec(p, l, 1)
            acc_vs = [a_.rearrange("(t p) d -> p t d", p=128) for a_ in acc_ds]
            load_gate(p, l, 1)
            p.add("dve", lambda e: e.memset(ss, 0.0), writes=["ss"])
            for t in range(NT):
                p.add("act", lambda e, t=t: e.activation(out=junk, in_=xs[:, t, :], func=AF.Square, accum_out=ss[:, t:t + 1]),
                      reads=["ss"], writes=[("ssc", t)])
            p.add("act", lambda e: e.activation(out=rstd, in_=ss, func=AF.Sqrt, bias=epst, scale=1.0 / D),
                  reads=[("ssc", t) for t in range(NT)], writes=["rstd"])
            p.add("dve", lambda e: e.reciprocal(out=rstd, in_=rstd), reads=["rstd"], writes=["rstd"])
            for t in range(NT):
                s2 = t % 2
                s4 = t % 4
                o_n = norm_chunk(p, t, rstd, tmps[0], ("tmp", 0), hcs[s4], ("hc", s4))
                if t == NT - 1:
                    xs_gate.append(o_n)
                p.dma("sp", lambda e, t=t, s4=s4: e.dma_start(out=hdr_d[t * 128:(t + 1) * 128, :], in_=hcs[s4]),
                      reads=[("hc", s4)], write=("hdr", t % 8))
                for kh in range(2):
                    pb = kh
                    def tr(e, s4=s4, kh=kh, pb=pb):
                        ins = None
                        for jj in range(4):
                            k = kh * 4 + jj
                            ins = e.transpose(out=PSb(pb, [128, 4, 128])[:, jj, :], in_=hcs[s4][:, k * 128:(k + 1) * 128], identity=ident)
                        return ins
                    p.add("pe", tr, reads=[("hc", s4)], writes=[("ps", pb)])
                    if kh == 0:
                        p.add("act", lambda e, s2=s2, pb=pb: e.copy(out=hTc[s2][:, 0:4, :], in_=PSb(pb, [128, 4, 128])),
                              reads=[("ps", pb)], writes=[("hTc", s2, 0)])
                    else:
                        p.add("dve", lambda e, s2=s2, pb=pb: e.tensor_copy(out=hTc[s2][:, 4:8, :], in_=PSb(pb, [128, 4, 128])),
                              reads=[("ps", pb)], writes=[("hTc", s2, 1)])
                _mm(p, PSf(2, [128, NT, NE])[:, t, :], [(hTc[s2][:, k, :], rwt[:, k, :]) for k in range(KC)],
                    reads=[("hTc", s2, 0), ("hTc", s2, 1), "rwt"], writes=[("lgps", t)])
                if has_mod and 2 <= t < 14:
                    mod_block(t - 2)
            p.add("pool", lambda e: e.memset(ysbs[0], 0.0), writes=[("ysb", 0)])
            for q in range(4):
                xs_gate.append(p.dma("sp", lambda e, q=q: e.dma_start(out=acc_vs[0][:, 4 * q:4 * q + 4, :], in_=xs[:, 4 * q:4 * q + 4, :]),
                                     write=("acc0", 0, q)))
            for q in range(8):
                p.dma("sp", lambda e, q=q: e.dma_start(out=acc_vs[1][:, 2 * q:2 * q + 2, :], in_=ysbs[0]),
                      reads=[("ysb", 0)], write=("acc0", 1, q % 4))
            lgk = [("lgps", t) for t in range(NT)]
            lg3 = PSf(2, [128, NT, NE])
            p.add("dve", lambda e: e.tensor_reduce(out=mxall, in_=lg3, axis=AX.X, op=ALU.max), reads=lgk, writes=["mxall"])
            p.add("dve", lambda e: e.tensor_tensor(out=lgs, in0=lg3, in1=mxall.unsqueeze(2).broadcast_to([128, NT, NE]), op=ALU.subtract),
                  reads=lgk + ["mxall"], writes=["lgs"])
            p.add("act", lambda e: e.activation(out=lgs, in_=lgs, func=AF.Exp), reads=["lgs"], writes=["lgs"])
            p.add("dve", lambda e: e.tensor_reduce(out=small, in_=lgs, axis=AX.X, op=ALU.add), reads=["lgs"], writes=["small"])
            p.add("dve", lambda e: e.reciprocal(out=small, in_=small), reads=["small"], writes=["small"])
            p.add("dve", lambda e: e.tensor_tensor(out=lgs, in0=lgs, in1=small.unsqueeze(2).broadcast_to([128, NT, NE]), op=ALU.mult),
                  reads=["lgs", "small"], writes=["lgs"])
            for t in range(NT):
                bk = 3 + t // 4
                p.add("pe", lambda e, t=t, bk=bk: e.transpose(out=PSf(bk, [NE, 512])[:, (t % 4) * 128:(t % 4 + 1) * 128], in_=lgs[:, t, :], identity=identf),
                      reads=["lgs"], writes=[("ps", bk)])
                if t % 4 == 3:
                    p.add("act", lambda e, t=t, bk=bk: e.copy(out=affT[:, (t // 4) * 512:(t // 4 + 1) * 512], in_=PSf(bk, [NE, 512])),
                          reads=[("ps", bk)], writes=[("affT", t // 4)])
            affk = [("affT", t) for t in range(4)]
            iot = AR.view(PH0 + 88 * K, [NE, S], I32)
            affT_i = affT.bitcast(I32)
            vals_i = vals.bitcast(I32)
            idxi = idxu.bitcast(I32)
            p.add("pool", lambda e: e.memset(mask_hi, -2048), writes=["mask_hi"])
            p.add("pool", lambda e: e.memset(mask_lo, 2047), writes=["mask_lo"])
            p.add("pool", lambda e: e.iota(iot, pattern=[[1, S]], base=0, channel_multiplier=0),
                  reads=[("tmp", 0), ("hc", 0), ("hc", 1)], writes=[("tmp", 0), ("hc", 0), ("hc", 1), "iot"])
            xs_open[0] = True
            pump()
            p.add("dve", lambda e: e.tensor_scalar(out=affT_i, in0=affT_i, scalar1=mask_hi, scalar2=None, op0=ALU.bitwise_and),
                  reads=affk + ["mask_hi"], writes=["affw"])
            p.add("dve", lambda e: e.tensor_tensor(out=affT_i, in0=affT_i, in1=iot, op=ALU.bitwise_or),
                  reads=["affw", "iot"], writes=["affw"])
            for i in range(CAP // 8):
                sl = slice(8 * i, 8 * i + 8)
                p.add("dve", lambda e, sl=sl: e.max(out=vals[:, sl], in_=affT), reads=["affw"], writes=[("vals", i)])
                p.add("dve", lambda e, sl=sl: e.match_replace(out=affT, in_to_replace=vals[:, sl], in_values=affT, imm_value=-1.0),
                      reads=[("vals", i), "affw"], writes=["affw"])
            vk = [("vals", i) for i in range(CAP // 8)]
            p.add("dve", lambda e: e.tensor_scalar(out=idxi, in0=vals_i, scalar1=mask_lo, scalar2=None, op0=ALU.bitwise_and),
                  reads=vk + ["mask_lo"], writes=["idxi"])
            p.add("dve", lambda e: e.tensor_copy(out=idxf, in_=idxi), reads=["idxi"], writes=["idxf"])
            p.add("dve", lambda e: e.tensor_scalar(out=vals_i, in0=vals_i, scalar1=mask_hi, scalar2=None, op0=ALU.bitwise_and),
                  reads=vk + ["idxi", "mask_hi"], writes=vk)
            for c2 in range(2):
                p.add("pe", lambda e, c2=c2: e.transpose(out=PSf(3 + c2, [128, NE]), in_=idxf[:, c2 * 128:(c2 + 1) * 128], identity=identf[0:NE, 0:NE]),
                      reads=["idxf"], writes=[("ps", 3 + c2)])
                p.add("dve", lambda e, c2=c2: e.tensor_copy(out=idx_tok[:, c2, :], in_=PSf(3 + c2, [128, NE])),
                      reads=[("ps", 3 + c2)], writes=[("idx_tok", c2)])
                p.add("pe", lambda e, c2=c2: e.transpose(out=PSf(5, [128, 2, NE])[:, c2, :], in_=vals[:, c2 * 128:(c2 + 1) * 128], identity=identf[0:NE, 0:NE]),
                      reads=[("vals", i) for i in range(CAP // 8)], writes=[("ps", 5)])
                p.add("act", lambda e, c2=c2: e.copy(out=gate_tok[:, c2, :], in_=PSf(5, [128, 2, NE])[:, c2, :]),
                      reads=[("ps", 5)], writes=[("gate_tok", c2)])
            if debug_out:
                dbg_i = nc.dram_tensor("dbg_idx", [128, 2 * NE], I32, kind="ExternalOutput").ap()
                dbg_g = nc.dram_tensor("dbg_gate", [128, 2 * NE], F32, kind="ExternalOutput").ap()
                dbg_a = nc.dram_tensor("dbg_vals", [NE, CAP], F32, kind="ExternalOutput").ap()
                dbg_u = nc.dram_tensor("dbg_idxu", [NE, CAP], U32, kind="ExternalOutput").ap()
                p.dma("sp", lambda e: e.dma_start(out=dbg_i, in_=idx_tok.rearrange("p a b -> p (a b)")), reads=[("idx_tok", 0), ("idx_tok", 1)], write="dbg_i")
                p.dma("sp", lambda e: e.dma_start(out=dbg_g, in_=gate_tok.rearrange("p a b -> p (a b)")), reads=[("gate_tok", 0), ("gate_tok", 1)], write="dbg_g")
                p.dma("sp", lambda e: e.dma_start(out=dbg_a, in_=vals), reads=[("vals", i) for i in range(CAP // 8)], write="dbg_a")
                p.dma("sp", lambda e: e.dma_start(out=dbg_u, in_=idxu), reads=["idxi"], write="dbg_u")
            hdr_keys = [("hdr", q) for q in range(8)]
            tk = RR([0, 1])

            def gather(ex):
                s3 = ex % 3
                for c2 in range(2):
                    p.dma("pool", lambda e, ex=ex, c2=c2, s3=s3: e.indirect_dma_start(
                        out=xes[s3][:, c2, :], out_offset=None, in_=hdr_d,
                        in_offset=bass.IndirectOffsetOnAxis(ap=idx_tok[:, c2, ex:ex + 1], axis=0)),
                        reads=hdr_keys + [("idx_tok", c2)], write=("xe", s3, c2))

            def transposes(ex):
                s2, s3 = ex % 2, ex % 3
                for c2 in range(2):
                    for kh in range(2):
                        pb = tk.next()
                        def tr(e, s3=s3, c2=c2, kh=kh, pb=pb):
                            ins = None
                            for jj in range(4):
                                k = kh * 4 + jj
                                ins = e.transpose(out=PSb(pb, [128, 4, 128])[:, jj, :], in_=xes[s3][:, c2, k * 128:(k + 1) * 128], identity=ident)
                            return ins
                        p.add("pe", tr, reads=[("xe", s3, c2)], writes=[("ps", pb)])
                        if kh == 0:
                            p.add("act", lambda e, s2=s2, c2=c2, pb=pb: e.copy(out=xeTs[s2][:, 0:4, c2 * 128:(c2 + 1) * 128], in_=PSb(pb, [128, 4, 128])),
                                  reads=[("ps", pb)], writes=[("xeT", s2, c2, 0)])
                        else:
                            p.add("dve", lambda e, s2=s2, c2=c2, pb=pb: e.tensor_copy(out=xeTs[s2][:, 4:8, c2 * 128:(c2 + 1) * 128], in_=PSb(pb, [128, 4, 128])),
                                  reads=[("ps", pb)], writes=[("xeT", s2, c2, 1)])

            gather(0)
            gather(1)
            transposes(0)
            scat = {}
            for ex in range(NE):
                s2 = ex % 2
                if ex + 2 < NE:
                    gather(ex + 2)
                xk = [("xeT", s2, c2, kh) for c2 in range(2) for kh in range(2)]
                base = EB + 6 * ex
                sl_ = [slot_of[base + i] for i in range(6)]
                for f in range(KC):
                    pa, pu = 2 + f % 2, 4 + f % 2
                    _mm(p, PSf(pa, [128, CAP]), [(wslr[sl_[k // 4]][:, k % 4, f * 128:(f + 1) * 128], xeTs[s2][:, k, :]) for k in range(KC)],
                        reads=xk + [("w", sl_[0]), ("w", sl_[1])], writes=[("ps", pa)])
                    _mm(p, PSf(pu, [128, CAP]), [(wslr[sl_[2 + k // 4]][:, k % 4, f * 128:(f + 1) * 128], xeTs[s2][:, k, :]) for k in range(KC)],
                        reads=xk + [("w", sl_[2]), ("w", sl_[3])], writes=[("ps", pu)])
                    sa = f % 2
                    p.add("act", lambda e, pa=pa, sa=sa: e.activation(out=sas[sa], in_=PSf(pa, [128, CAP]), func=AF.Silu),
                          reads=[("ps", pa)], writes=[("sa", sa)])
                    p.add("dve", lambda e, pu=pu, sa=sa, s2=s2, f=f: e.tensor_tensor(out=hids[s2][:, f, :], in0=sas[sa], in1=PSf(pu, [128, CAP]), op=ALU.mult),
                          reads=[("ps", pu), ("sa", sa)], writes=[("hid", f)])
                consumed(base + 3)
                if ex + 1 < NE:
                    transposes(ex + 1)
                hk = [("hid", f) for f in range(KC)]
                for hf in range(2):
                    for c2 in range(2):
                        py = 6 + (hf * 2 + c2) % 2
                        _mm(p, PSf(py), [(hids[s2][:, f, c2 * 128:(c2 + 1) * 128], wslr[sl_[4 + f // 4]][:, f % 4, hf * 512:(hf + 1) * 512]) for f in range(KC)],
                            reads=hk + [("w", sl_[4]), ("w", sl_[5])], writes=[("ps", py)])
                        p.add("act", lambda e, py=py, s2=s2, c2=c2, hf=hf, ex=ex: e.activation(
                            out=ysbs[s2][:, c2, hf * 512:(hf + 1) * 512], in_=PSf(py), func=AF.Copy, scale=gate_tok[:, c2, ex:ex + 1]),
                            reads=[("ps", py), ("gate_tok", c2)], writes=[("ysb", s2)])
                consumed(base + 5)
                for c2 in range(2):
                    p.add("dve", lambda e, s2=s2, c2=c2: e.tensor_tensor(out=ysbs[s2][:, c2, :], in0=ysbs[s2][:, c2, :], in1=mvC, op=ALU.mult),
                          reads=[("ysb", s2), "mvC"], writes=[("ysb", s2)])
                a_i = ex % 2
                for c2 in range(2):
                    prev_other = scat.get((ex - 2, 1 - c2))
                    scat[(ex, c2)] = p.dma("pool", lambda e, ex=ex, c2=c2, s2=s2, a_i=a_i: e.indirect_dma_start(
                        out=acc_ds[a_i], out_offset=bass.IndirectOffsetOnAxis(ap=idx_tok[:, c2, ex:ex + 1], axis=0),
                        in_=ysbs[s2][:, c2, :], in_offset=None, compute_op=ALU.add),
                        reads=[("ysb", s2), ("idx_tok", c2)] + [("acc0", a_i, q) for q in range(4)], write=("accw", a_i, c2),
                        after=[prev_other])
            acck = [("accw", a_i, c2) for a_i in range(2) for c2 in range(2)]
            cb = Bump(AR, PH0 + 80 * K, PH0 + 104 * K)
            cbufs = [(ysbs[0], ysbs[1], ("ysb", 0), ("ysb", 1)),
                     (cb.alloc([128, 2, D], F32), cb.alloc([128, 2, D], F32), ("cb", 0), ("cb", 1))]
            for q in range(8):
                bA, bB, kA, kB = cbufs[q % 2]
                p.dma("sp", lambda e, q=q, bA=bA: e.dma_start(out=bA, in_=acc_vs[0][:, 2 * q:2 * q + 2, :]), reads=acck, write=kA)
                p.dma("sp", lambda e, q=q, bB=bB: e.dma_start(out=bB, in_=acc_vs[1][:, 2 * q:2 * q + 2, :]), reads=acck, write=kB)
                p.add("dve", lambda e, q=q, bA=bA, bB=bB: e.tensor_tensor(out=xs[:, 2 * q:2 * q + 2, :], in0=bA, in1=bB, op=ALU.add),
                      reads=[kA, kB], writes=[("xs", 2 * q), ("xs", 2 * q + 1), kA, kB])
            if last:
                for q4 in range(4):
                    p.dma("sp", lambda e, q4=q4: e.dma_start(out=out_v[:, q4 * 4:(q4 + 1) * 4, :], in_=xs[:, q4 * 4:(q4 + 1) * 4, :]),
                          reads=[("xs", t) for t in range(q4 * 4, q4 * 4 + 4)], write=("out", q4))
            p.finish()

        def phase_store():
            p = Prog(nc, ctx)
            for q4 in range(4):
                p.dma("sp", lambda e, q4=q4: e.dma_start(out=out_v[:, q4 * 4:(q4 + 1) * 4, :], in_=xs[:, q4 * 4:(q4 + 1) * 4, :]), write=("out", q4))
            p.finish()

        stored = False
        for i, ph in enumerate(phases):
            if ph[0] == "mod":
                phase_mod()
            elif ph[0] == "mix":
                if ph[1] % 2 == 0:
                    phase_mix_even(ph[1])
                else:
                    phase_mix_odd(ph[1])
            elif ph[0] == "moe":
                last = (i == len(phases) - 1)
                phase_moe(ph[1], last)
                stored = stored or last
        if not stored:
            phase_store()
    return nc


_CONST = {}


def _constants():
    if not _CONST:
        s = np.arange(S, dtype=np.int64)
        ang = 2.0 * np.pi * ((s[:, None] * s[None, :]) % S).astype(np.float64) / S
        _CONST["dft_cos"] = (np.cos(ang) / np.sqrt(S)).astype(ml_dtypes.bfloat16)
        _CONST["dft_sin"] = (np.sin(ang) / np.sqrt(S)).astype(ml_dtypes.bfloat16)
        c = np.arange(256, dtype=np.int64)
        ang = 2.0 * np.pi * ((c[:, None] * c[None, :]) % 256).astype(np.float64) / 256
        _CONST["dft_cc"] = (np.cos(ang) / 16.0).astype(ml_dtypes.bfloat16)
        _CONST["dft_csn"] = (-np.sin(ang) / 16.0).astype(ml_dtypes.bfloat16)
        sgn = np.where(np.arange(S) % 2 == 0, 1.0, -1.0) / np.sqrt(S)
        _CONST["dft_nyq"] = np.ascontiguousarray(sgn.reshape(NT, 128).T).astype(ml_dtypes.bfloat16)
        _CONST["bidx"] = _bias_index()
    return _CONST


def make_in_maps(inputs):
    cst = _constants()
    f = lambda a: np.ascontiguousarray(np.asarray(a, dtype=np.float32))
    rpb = f(inputs["rpb"])
    dr, dc, ok = cst["bidx"]
    btab = np.empty((2, 8, 128, 5, 5, 128), np.float32)
    for j in range(2):
        g = rpb[j][:, dr, dc]
        g = np.where(ok[None], g, np.float32(NEG))
        btab[j] = g.transpose(0, 2, 1, 3, 4)
    btab = np.ascontiguousarray(btab.reshape(2, 8, 128, 3200))
    qk_g = f(inputs["qk_g"])
    qkg = np.ascontiguousarray(np.concatenate([qk_g, qk_g], axis=2).transpose(0, 2, 1))
    conv_w = f(inputs["conv_w"])
    cw = np.ascontiguousarray(conv_w.reshape(2, 3, 4, 128).transpose(0, 3, 2, 1).reshape(2, 128, 12))
    shared = {
        "mod_w": f(inputs["mod_w"]), "mod_b": f(inputs["mod_b"]),
        "norm_g": f(inputs["norm_g"]).reshape(DEPTH * 2, D),
        "mix_w_in": f(inputs["mix_w_in"]), "mix_w_out": f(inputs["mix_w_out"]), "fnet_w_out": f(inputs["fnet_w_out"]),
        "qkg": qkg, "cw": cw, "btab": btab, "router_w": f(inputs["router_w"]),
        "exp_w_gate": f(inputs["exp_w_gate"]), "exp_w_up": f(inputs["exp_w_up"]), "exp_w_down": f(inputs["exp_w_down"]),
        "dft_cos": cst["dft_cos"], "dft_sin": cst["dft_sin"], "dft_cc": cst["dft_cc"], "dft_csn": cst["dft_csn"], "dft_nyq": cst["dft_nyq"],
    }
    x = f(inputs["x"])
    c = f(inputs["c"])
    maps = []
    for bidx in range(x.shape[0]):
        m = dict(shared)
        m["x"] = x[bidx]
        m["cT"] = np.ascontiguousarray(c[bidx].reshape(8, 128).T)
        maps.append(m)
    return maps


def kernel(**inputs):
    nc = build_program()
    maps = make_in_maps(inputs)
    res = run_bass_kernel_spmd(nc, maps, core_ids=list(range(len(maps))))
    return np.stack([r["out"] for r in res.results], axis=0).astype(np.float32)
```
